# Optimizing a Trainium2 kernel written in Bass

```python
import jax, jax.numpy as jnp
from jax import lax
import numpy as np

D_MODEL = 2048
BATCH = 4
SEQ = 2048
DEPTH = 2
DEC_BATCH = 128
DEC_SEQ = 1
PAST_LEN = 16384
PAGE_SIZE = 128

N_MIXERS = 2
CHUNK = 128
E_A = D_MODEL
H_A = 8
DH_A = E_A // H_A
E_B = D_MODEL
CONV_K = 3
D_FF = ((8 * D_MODEL // 3 + 255) // 256) * 256
D_PLE = 256
N_A = (DEPTH + 1) // 2
N_B = DEPTH // 2
EPS = 1e-6

kernel_name = "hybrid_gmlp_shortconv_macaron_decode_step"


def _rmsnorm(x, g):
    xf = x.astype(jnp.float32)
    y = xf * lax.rsqrt(jnp.mean(xf * xf, axis=-1, keepdims=True) + EPS)
    return (y * g.astype(jnp.float32)).astype(x.dtype)


def _layernorm(x, g, b):
    xf = x.astype(jnp.float32)
    mu = jnp.mean(xf, axis=-1, keepdims=True)
    var = jnp.mean(jnp.square(xf - mu), axis=-1, keepdims=True)
    y = (xf - mu) * lax.rsqrt(var + EPS)
    return (y * g.astype(jnp.float32) + b.astype(jnp.float32)).astype(x.dtype)


def _swiglu(x, w_gate, w_up, w_down):
    return (jax.nn.silu(x @ w_gate) * (x @ w_up)) @ w_down


def _chunk_gmlp_mixer(a, w_in, ln_g, ln_b, w_s, b_s, w_out):
    n, L, _ = a.shape
    c = min(L, CHUNK)
    z = jax.nn.gelu(a @ w_in)
    u, v = jnp.split(z, 2, axis=-1)
    v = _layernorm(v, ln_g, ln_b)
    mask = jnp.tril(jnp.ones((c, c), dtype=bool))
    w = w_s[:, :c, :c]
    w = jnp.where(mask[None], w, jnp.zeros_like(w)).astype(v.dtype)
    vg = v.reshape(n, L // c, c, H_A, DH_A)
    s = jnp.einsum('hts,nkshd->nkthd', w, vg) + b_s[:, :c].T.astype(v.dtype)[None, None, :, :, None]
    y = u * s.reshape(n, L, E_A)
    return y @ w_out, v


def _short_conv_mixer(a, conv_state, w_in, w_conv, w_out):
    L = a.shape[1]
    bg, cg, xh = jnp.split(a @ w_in, 3, axis=-1)
    ci = cg * xh
    xp = jnp.concatenate([conv_state.astype(ci.dtype), ci], axis=1)
    co = w_conv[0] * xp[:, 0:L]
    for k in range(1, CONV_K):
        co = co + w_conv[k] * xp[:, k:k + L]
    return (bg * co) @ w_out, xp[:, L:]


def _trunk(x, p, conv_state, W):
    h = x
    conv_new, v_new = [], []
    for i in range(DEPTH):
        h = h + 0.5 * _swiglu(_rmsnorm(h, W['ffn1_norm'][i]), W['ffn1_w_gate'][i], W['ffn1_w_up'][i], W['ffn1_w_down'][i])
        a = _rmsnorm(h, W['mix_norm'][i])
        j = i // N_MIXERS
        if i % N_MIXERS == 0:
            m, v = _chunk_gmlp_mixer(a, W['a_w_in'][j], W['a_ln_g'][j], W['a_ln_b'][j], W['a_w_s'][j], W['a_b_s'][j], W['a_w_out'][j])
            v_new.append(v)
        else:
            st = conv_state[j] if conv_state is not None else jnp.zeros((x.shape[0], CONV_K - 1, E_B), x.dtype)
            m, st_new = _short_conv_mixer(a, st, W['c_w_in'][j], W['c_w_conv'][j], W['c_w_out'][j])
            conv_new.append(st_new)
        h = h + m
        h = h + 0.5 * _swiglu(_rmsnorm(h, W['ffn2_norm'][i]), W['ffn2_w_gate'][i], W['ffn2_w_up'][i], W['ffn2_w_down'][i])
        gate = jax.nn.sigmoid(_rmsnorm(h, W['ple_norm'][i]) @ W['ple_w_gate'][i])
        h = h + gate * (p[i] @ W['ple_w_proj'][i])
    y = _rmsnorm(h, W['final_norm'])
    return y, jnp.stack(conv_new), jnp.stack(v_new)


def setup_inputs(seed: int = 0) -> dict:
    key = jax.random.key(seed)
    ks = iter(jax.random.split(key, 40))

    def nrm(shape, scale):
        return jax.random.normal(next(ks), shape, jnp.float32) * scale

    def gain(shape):
        return 1.0 + 0.05 * jax.random.normal(next(ks), shape, jnp.float32)

    return {
        "x_prompt": nrm((BATCH, SEQ, D_MODEL), 1.0),
        "x_sample": nrm((DEC_BATCH, DEC_SEQ, D_MODEL), 1.0),
        "state_conv": nrm((N_B, DEC_BATCH, CONV_K - 1, E_B), 1.0),
        "p_prompt": nrm((DEPTH, BATCH, SEQ, D_PLE), 1.0),
        "p_sample": nrm((DEPTH, DEC_BATCH, DEC_SEQ, D_PLE), 1.0),
        "ffn1_norm": gain((DEPTH, D_MODEL)),
        "ffn1_w_gate": nrm((DEPTH, D_MODEL, D_FF), D_MODEL ** -0.5),
        "ffn1_w_up": nrm((DEPTH, D_MODEL, D_FF), D_MODEL ** -0.5),
        "ffn1_w_down": nrm((DEPTH, D_FF, D_MODEL), D_FF ** -0.5),
        "mix_norm": gain((DEPTH, D_MODEL)),
        "a_w_in": nrm((N_A, D_MODEL, 2 * E_A), D_MODEL ** -0.5),
        "a_ln_g": gain((N_A, E_A)),
        "a_ln_b": nrm((N_A, E_A), 0.02),
        "a_w_s": nrm((N_A, H_A, CHUNK, CHUNK), CHUNK ** -0.5),
        "a_b_s": 1.0 + nrm((N_A, H_A, CHUNK), 0.1),
        "a_w_out": nrm((N_A, E_A, D_MODEL), E_A ** -0.5),
        "c_w_in": nrm((N_B, D_MODEL, 3 * E_B), D_MODEL ** -0.5),
        "c_w_conv": nrm((N_B, CONV_K, E_B), CONV_K ** -0.5),
        "c_w_out": nrm((N_B, E_B, D_MODEL), E_B ** -0.5),
        "ffn2_norm": gain((DEPTH, D_MODEL)),
        "ffn2_w_gate": nrm((DEPTH, D_MODEL, D_FF), D_MODEL ** -0.5),
        "ffn2_w_up": nrm((DEPTH, D_MODEL, D_FF), D_MODEL ** -0.5),
        "ffn2_w_down": nrm((DEPTH, D_FF, D_MODEL), D_FF ** -0.5),
        "ple_norm": gain((DEPTH, D_MODEL)),
        "ple_w_gate": nrm((DEPTH, D_MODEL, D_MODEL), D_MODEL ** -0.5),
        "ple_w_proj": nrm((DEPTH, D_PLE, D_MODEL), D_PLE ** -0.5),
        "final_norm": gain((D_MODEL,)),
    }


def reference(x_prompt, x_sample, state_conv, p_prompt, p_sample,
              ffn1_norm, ffn1_w_gate, ffn1_w_up, ffn1_w_down,
              mix_norm, a_w_in, a_ln_g, a_ln_b, a_w_s, a_b_s, a_w_out,
              c_w_in, c_w_conv, c_w_out,
              ffn2_norm, ffn2_w_gate, ffn2_w_up, ffn2_w_down,
              ple_norm, ple_w_gate, ple_w_proj, final_norm):
    W = {
        'ffn1_norm': ffn1_norm, 'ffn1_w_gate': ffn1_w_gate, 'ffn1_w_up': ffn1_w_up, 'ffn1_w_down': ffn1_w_down,
        'mix_norm': mix_norm, 'a_w_in': a_w_in, 'a_ln_g': a_ln_g, 'a_ln_b': a_ln_b,
        'a_w_s': a_w_s, 'a_b_s': a_b_s, 'a_w_out': a_w_out,
        'c_w_in': c_w_in, 'c_w_conv': c_w_conv, 'c_w_out': c_w_out,
        'ffn2_norm': ffn2_norm, 'ffn2_w_gate': ffn2_w_gate, 'ffn2_w_up': ffn2_w_up, 'ffn2_w_down': ffn2_w_down,
        'ple_norm': ple_norm, 'ple_w_gate': ple_w_gate, 'ple_w_proj': ple_w_proj, 'final_norm': final_norm,
    }
    y_prompt, new_conv_prompt, _ = _trunk(x_prompt, p_prompt, None, W)
    y_sample, new_conv_sample, new_chunk_v_sample = _trunk(x_sample, p_sample, state_conv, W)
    return (y_prompt, y_sample, new_conv_prompt, new_conv_sample, new_chunk_v_sample)
```

```python
import contextlib
import numpy as np
import concourse.bass as bass
import concourse.mybir as mybir
from concourse.bass_utils import run_bass_kernel_spmd

F32 = mybir.dt.float32
F32R = mybir.dt.float32r
BF16 = mybir.dt.bfloat16
F16 = mybir.dt.float16
AF = mybir.ActivationFunctionType
ALU = mybir.AluOpType

D = 2048
DFF = 5632
NCH = 16
NF = 44
DPLE = 256
TH = 1168
NMAIN = 1024
NSAMP = 16
TT_FULL = [(0, 390), (390, 390), (780, 388)]
TT = [(126, 348), (474, 348), (822, 346)]
C_MAIN = 128
C_SAMP = 1152
PIECES = [(0, 15), (15, 30), (30, 44)]
NSLOT = 3
EPS = 1e-6
N_CORES = 8

G_FFN1, G_MIX, G_FFN2, G_PLE = 0, 2, 4, 6
G_LNG, G_LNB, G_WC = 8, 9, 10
NGRP = 13


class Sync:
    def __init__(self, nc, es):
        self.nc = nc
        self.es = es
        self.engs = {"pe": nc.tensor, "act": nc.scalar, "dve": nc.vector, "pool": nc.gpsimd, "sp": nc.sync}
        self.sems = {}
        self.cnt = {}
        for e in ("pe", "act", "dve"):
            self.sems[e] = es.enter_context(nc.semaphore("s_" + e))
            self.cnt[e] = 0
        self.known = {e: {} for e in self.engs}
        self.last_w = {}
        self.readers = {}
        self.pending = {e: {} for e in self.engs}
        self.dma_events = {}

    def _sem(self, key):
        if key not in self.sems:
            self.sems[key] = self.es.enter_context(self.nc.semaphore("s_" + "_".join(str(k) for k in key)))
            self.cnt[key] = 0
        return self.sems[key]

    def _waits(self, eng, reads, writes):
        deps = dict(self.pending[eng])
        self.pending[eng] = {}

        def add(ev):
            k, v = ev
            if deps.get(k, 0) < v:
                deps[k] = v

        for r in reads:
            if r in self.last_w:
                add(self.last_w[r])
        for w in writes:
            if w in self.last_w:
                add(self.last_w[w])
            for ev in self.readers.get(w, {}).items():
                add(ev)
        kn = self.known[eng]
        for k, v in deps.items():
            if k == "pe" and eng == "pe":
                continue
            if kn.get(k, 0) >= v:
                continue
            self.engs[eng].wait_ge(self.sems[k], v)
            kn[k] = v

    def _record(self, ev, reads, writes):
        for r in reads:
            d = self.readers.setdefault(r, {})
            if d.get(ev[0], 0) < ev[1]:
                d[ev[0]] = ev[1]
        for w in writes:
            self.last_w[w] = ev
            self.readers[w] = {}

    @staticmethod
    def _flat(lst):
        out = []
        for r in lst:
            if isinstance(r, list):
                out.extend(r)
            else:
                out.append(r)
        return out

    def op(self, eng, reads, writes, emit):
        reads, writes = self._flat(reads), self._flat(writes)
        self._waits(eng, reads, writes)
        inst = emit()
        self.cnt[eng] += 1
        inst.then_inc(self.sems[eng], 1)
        self._record((eng, self.cnt[eng]), reads, writes)

    def dma(self, queue, reads, writes, semkey, emit):
        reads, writes = self._flat(reads), self._flat(writes)
        self._sem(semkey)
        self._waits(queue, reads, writes)
        inst = emit()
        self.cnt[semkey] += 16
        inst.then_inc(self.sems[semkey], 16)
        ev = (semkey, self.cnt[semkey])
        self._record(ev, reads, writes)
        if semkey[0] != "wld":
            self.dma_events[semkey] = self.cnt[semkey]

    def barrier(self, engs=("pe", "act", "dve", "sp")):
        snap = {e: self.cnt[e] for e in ("pe", "act", "dve") if self.cnt[e] > 0}
        snap.update(self.dma_events)
        for e in engs:
            p = self.pending[e]
            for k, v in snap.items():
                if k == e:
                    continue
                if p.get(k, 0) < v:
                    p[k] = v

    def finish(self):
        self.barrier()
        self._waits("sp", [], [])


def build_program():
    nc = bass.Bass("TRN2", target_bir_lowering=False)

    def din(name, shape):
        return nc.dram_tensor(name, list(shape), F32, kind="ExternalInput").ap()

    def dout(name, shape):
        return nc.dram_tensor(name, list(shape), F32, kind="ExternalOutput").ap()

    xin = din("xin", (TH, D))
    pin = din("pin", (2, TH, DPLE))
    stc = din("stc", (NSAMP, 2, D))
    cols = din("cols", (128, NGRP * 16))
    identd = din("ident", (128, 128))
    trid = din("tri", (128, 128))
    wd = {}
    for nm in ("ffn1", "ffn2"):
        wd[nm + "_g"] = din(nm + "_w_gate", (2, D, DFF))
        wd[nm + "_u"] = din(nm + "_w_up", (2, D, DFF))
        wd[nm + "_d"] = din(nm + "_w_down", (2, DFF, D))
    wd["a_in"] = din("a_w_in", (1, D, 2 * D))
    wd["a_out"] = din("a_w_out", (1, D, D))
    wd["c_in"] = din("c_w_in", (1, D, 3 * D))
    wd["c_out"] = din("c_w_out", (1, D, D))
    wd["ple_g"] = din("ple_w_gate", (2, D, D))
    wd["ple_p"] = din("ple_w_proj", (2, DPLE, D))
    a_w_s = din("a_w_s", (1, 8, 128, 128))
    a_b_s = din("a_b_s", (1, 8, 128))
    final_norm = din("final_norm", (D,))

    y_out = dout("y", (NMAIN + NSAMP, D))
    ncp_out = dout("ncp", (2, D))
    ncs_out = dout("ncs", (NSAMP, 2, D))
    vs_out = dout("vs", (NSAMP, D))

    plan = []

    def P(name, l, k0, nk, c0):
        plan.append((name, l, k0, nk, c0))

    def plan_ffn(nm, l):
        for (f0, f1) in PIECES:
            for f in range(f0, f1):
                P(nm + "_g", l, 0, 16, f * 128)
                P(nm + "_u", l, 0, 16, f * 128)
            for d in range(NCH):
                P(nm + "_d", l, f0, f1 - f0, d * 128)

    def plan_ple(l):
        for d in range(NCH):
            P("ple_g", l, 0, 16, d * 128)
            P("ple_p", l, 0, 2, d * 128)

    plan_ffn("ffn1", 0)
    for c in range(NCH):
        P("a_in", 0, 0, 16, D + c * 128)
    for c in range(NCH):
        P("a_in", 0, 0, 16, c * 128)
    for d in range(NCH):
        P("a_out", 0, 0, 16, d * 128)
    plan_ffn("ffn2", 0)
    plan_ple(0)
    plan_ffn("ffn1", 1)
    for c in range(NCH):
        P("c_in", 0, 0, 16, D + c * 128)
        P("c_in", 0, 0, 16, 2 * D + c * 128)
        P("c_in", 0, 0, 16, c * 128)
    for d in range(NCH):
        P("c_out", 0, 0, 16, d * 128)
    plan_ffn("ffn2", 1)
    plan_ple(1)

    uid = [0]

    def sbt(name, shape, dt):
        uid[0] += 1
        return nc.sbuf_tensor("%s_u%d" % (name, uid[0]), shape, dt)

    def pmt(name, shape, dt):
        uid[0] += 1
        return nc.psum_tensor("%s_u%d" % (name, uid[0]), shape, dt)

    with contextlib.ExitStack() as es:
        S = Sync(nc, es)
        E = es.enter_context

        h = E(sbt("h", [128, NCH, TH], F32))
        colt = E(sbt("colt", [128, NGRP * 16], F32))
        ident_f = E(sbt("ident_f", [128, 128], F32))
        ident_b = E(sbt("ident_b", [128, 128], BF16))
        tri_f = E(sbt("tri_f", [128, 128], F32))
        ones_f = E(sbt("ones_f", [128, 128], F32))
        ones_b = E(sbt("ones_b", [128, 128], BF16))
        ones_h = E(sbt("ones_h", [128, 128], F16))
        w00b = E(sbt("w00b", [128, 8, 1], F32))
        b0b = E(sbt("b0b", [128, 8, 1], F32))

        banks = [E(pmt("bank%d" % i, [128, 512], F32)) for i in range(8)]
        PA, PB, SP0, SP1 = banks[0:3], banks[3:6], banks[6], banks[7]
        PA_R, PB_R = ["A0", "A1", "A2"], ["B0", "B1", "B2"]

        def v4(bank):
            return bank[:].rearrange("p (a b) -> p a b", a=4)

        def gcol(g, c):
            return colt[:, g * 16 + c:g * 16 + c + 1]

        S.dma("sp", [], ["c_colt"], ("cst", 0), lambda: nc.sync.dma_start(out=colt[:], in_=cols))
        S.dma("sp", [], ["c_identf"], ("cst", 1), lambda: nc.sync.dma_start(out=ident_f[:], in_=identd))
        S.dma("sp", [], ["c_tri"], ("cst", 2), lambda: nc.sync.dma_start(out=tri_f[:], in_=trid))
        S.dma("sp", [], ["c_w00"], ("cst", 3), lambda: nc.sync.dma_start(
            out=w00b[:], in_=bass.AP(a_w_s.tensor, 0, [[0, 128], [128 * 128, 8], [1, 1]]),
            allow_slow_non_contiguous=True))
        S.dma("sp", [], ["c_b0"], ("cst", 4), lambda: nc.sync.dma_start(
            out=b0b[:], in_=bass.AP(a_b_s.tensor, 0, [[0, 128], [128, 8], [1, 1]]),
            allow_slow_non_contiguous=True))
        S.op("dve", [], ["c_onesf"], lambda: nc.vector.memset(ones_f[:], 1.0))
        S.op("dve", [], ["c_onesb"], lambda: nc.vector.memset(ones_b[:], 1.0))
        S.op("dve", [], ["c_onesh"], lambda: nc.vector.memset(ones_h[:], 1.0))
        S.op("dve", ["c_identf"], ["c_identb"], lambda: nc.vector.tensor_copy(out=ident_b[:], in_=ident_f[:]))
        S.barrier(engs=("pe", "act", "dve"))

        def load_T(tag, src_fn, blocks, nch, dest, dest_c0, width, tok=None, nbank=2, ntok=2):
            with contextlib.ExitStack() as ls:
                own = tok is None
                if own:
                    tok = [ls.enter_context(sbt("%s_tok%d" % (tag, i), [128, width], F32)) for i in range(ntok)]
                bank_ids = [6, 7, 0, 1, 2, 3, 4, 5][:nbank]
                bank_res = {6: "S0", 7: "S1", 0: "A0", 1: "A1", 2: "A2", 3: "B0", 4: "B1", 5: "B2"}
                pt = [v4(banks[b]) for b in bank_ids]
                ptr = [bank_res[b] for b in bank_ids]
                qi = 0
                for bi, (r0, nr, c0) in enumerate(blocks):
                    tb = tok[bi % len(tok)]
                    S.dma("sp", [], [(tag, "tok", bi % len(tok))], (tag, "ld", bi % len(tok)),
                          lambda tb=tb, r0=r0, nr=nr: nc.sync.dma_start(out=tb[0:nr, 0:nch * 128], in_=src_fn(r0, nr)))
                    for q in range((nch + 3) // 4):
                        ncc = min(4, nch - q * 4)
                        pq = pt[qi % nbank]
                        pqr = ptr[qi % nbank]

                        def emit_t(tb=tb, pq=pq, q=q, ncc=ncc, nr=nr):
                            last = None
                            for cc in range(ncc):
                                c = q * 4 + cc
                                last = nc.tensor.transpose(out=pq[:, cc, 0:nr], in_=tb[0:nr, c * 128:(c + 1) * 128],
                                                           identity=ident_f[0:nr, 0:nr])
                            return last

                        S.op("pe", [(tag, "tok", bi % len(tok))], [pqr], emit_t)
                        dres = [(dest[1], dest_c0 + q * 4 + cc) for cc in range(ncc)]
                        dap = dest[0][:, dest_c0 + q * 4:dest_c0 + q * 4 + ncc, c0:c0 + nr]
                        if qi % 2 == 0:
                            S.op("act", [pqr], dres,
                                 lambda dap=dap, pq=pq, ncc=ncc, nr=nr: nc.scalar.activation(
                                     out=dap, in_=pq[:, 0:ncc, 0:nr], func=AF.Copy))
                        else:
                            S.op("dve", [pqr], dres,
                                 lambda dap=dap, pq=pq, ncc=ncc, nr=nr: nc.vector.tensor_copy(
                                     out=dap, in_=pq[:, 0:ncc, 0:nr]))
                        qi += 1
                if own:
                    S.barrier()

        xblocks = [(j * 128, 128, j * 128) for j in range(9)] + [(1152, 16, 1152)]
        load_T("x", lambda r0, nr: xin[r0:r0 + nr, :], xblocks, NCH, (h, "h"), 0, D, nbank=8, ntok=4)

        with contextlib.ExitStack() as ms:
            M = ms.enter_context
            xn = M(sbt("xn", [128, NCH, TH], BF16))
            buf2 = M(sbt("buf2", [128, NCH, TH], BF16))
            wslots = [M(sbt("wsl%d" % i, [128, 16, 128], BF16)) for i in range(NSLOT)]

            XN_ALL = [("xn", c, ti) for c in range(NCH) for ti in range(3)]

            wstate = {"issued": 0, "used": 0}

            def wsrc(ent):
                name, l, k0, nk, c0 = ent
                return wd[name][l].rearrange("(k p) c -> p k c", p=128)[:, k0:k0 + nk, c0:c0 + 128]

            def wpump():
                while wstate["issued"] < min(len(plan), wstate["used"] + NSLOT):
                    i = wstate["issued"]
                    s = i % NSLOT
                    ent = plan[i]
                    nk = ent[3]
                    S.dma("pool", [], [("w", s)], ("wld", s),
                          lambda s=s, nk=nk, ent=ent: nc.gpsimd.dma_start(out=wslots[s][:, 0:nk, :], in_=wsrc(ent)))
                    wstate["issued"] += 1

            def wget(name, l, k0, nk, c0):
                i = wstate["used"]
                assert plan[i] == (name, l, k0, nk, c0), (i, plan[i], (name, l, k0, nk, c0))
                wpump()
                wstate["used"] += 1
                s = i % NSLOT
                return wslots[s], ("w", s)

            def mm_group(ws, wres, nk, rhs_fn, tts, pset, pres, reads, tile_major=False, split_last=False):
                if split_last:
                    def emit_a():
                        last = None
                        for k in range(nk - 1):
                            for ti, (t0, tn) in enumerate(tts):
                                last = nc.tensor.matmul(pset[ti][:, 0:tn], lhsT=ws[:, k, :], rhs=rhs_fn(k, t0, tn),
                                                        start=(k == 0), stop=False)
                        return last
                    S.op("pe", [wres] + reads[:-1], [pres], emit_a)

                    def emit_b():
                        last = None
                        for ti, (t0, tn) in enumerate(tts):
                            last = nc.tensor.matmul(pset[ti][:, 0:tn], lhsT=ws[:, nk - 1, :], rhs=rhs_fn(nk - 1, t0, tn),
                                                    start=False, stop=True)
                        return last
                    S.op("pe", [wres, reads[-1]], [pres], emit_b)
                    return
                if tile_major:
                    for ti, (t0, tn) in enumerate(tts):
                        def emit_tm(ti=ti, t0=t0, tn=tn):
                            last = None
                            for k in range(nk):
                                last = nc.tensor.matmul(pset[ti][:, 0:tn], lhsT=ws[:, k, :], rhs=rhs_fn(k, t0, tn),
                                                        start=(k == 0), stop=(k == nk - 1))
                            return last
                        S.op("pe", [wres] + [("xn", c, ti) for c in range(NCH)], [pres[ti]], emit_tm)
                    return

                def emit():
                    last = None
                    for k in range(nk):
                        for ti, (t0, tn) in enumerate(tts):
                            last = nc.tensor.matmul(pset[ti][:, 0:tn], lhsT=ws[:, k, :], rhs=rhs_fn(k, t0, tn),
                                                    start=(k == 0), stop=(k == nk - 1))
                    return last
                S.op("pe", [wres] + reads, [pres], emit)

            def xn_rhs(k, t0, tn):
                return xn[:, k, t0:t0 + tn]

            def b2_rhs(k, t0, tn):
                return buf2[:, k, t0:t0 + tn]

            sqA = [M(sbt("sqA%d" % i, [128, 512], F16)) for i in range(2)]
            sqD = [M(sbt("sqD%d" % i, [128, 512], F16)) for i in range(2)]
            rstdF = M(sbt("rstdF", [128, TH], F32))
            RSTD_ALL = [("rstdF", gi) for gi in range(3)]
            nstate = {"active": None, "pending": None}

            def stat_groups(tts):
                lo = tts[0][0]
                hi = tts[2][0] + tts[2][1]
                return [(lo, 512), (lo + 512, 512), (lo + 1024, hi - lo - 1024)]

            def norm_chunk(c, spec, xn_dve=()):
                g, tts, with_xn = spec
                (a0, an), (b0, bn), _ = stat_groups(tts)
                S.op("act", [("h", c)], [("sqA", c % 2)], lambda: nc.scalar.activation(
                    out=sqA[c % 2][:, 0:an], in_=h[:, c, a0:a0 + an], func=AF.Square, scale=0.0625))
                S.op("pe", [("sqA", c % 2)], ["S0"], lambda: nc.tensor.matmul(
                    SP0[:, 0:an], lhsT=ones_h[:], rhs=sqA[c % 2][:, 0:an], start=(c == 0), stop=(c == NCH - 1)))
                S.op("dve", [("h", c)], [("sqD", c % 2)], lambda: nc.vector.scalar_tensor_tensor(
                    out=sqD[c % 2][:, 0:bn], in0=h[:, c, b0:b0 + bn], scalar=1.0 / 256, in1=h[:, c, b0:b0 + bn],
                    op0=ALU.mult, op1=ALU.mult))
                S.op("pe", [("sqD", c % 2)], ["S1"], lambda: nc.tensor.matmul(
                    SP1[:, 0:bn], lhsT=ones_h[:], rhs=sqD[c % 2][:, 0:bn], start=(c == 0), stop=(c == NCH - 1)))
                if with_xn:
                    for ti, (t0, tn) in enumerate(tts):
                        if ti in xn_dve:
                            S.op("dve", [("h", c)], [("xn", c, ti)], lambda: nc.vector.tensor_scalar(
                                out=xn[:, c, t0:t0 + tn], in0=h[:, c, t0:t0 + tn], scalar1=gcol(g, c), scalar2=None,
                                op0=ALU.mult))
                        else:
                            S.op("act", [("h", c)], [("xn", c, ti)], lambda: nc.scalar.activation(
                                out=xn[:, c, t0:t0 + tn], in_=h[:, c, t0:t0 + tn], func=AF.Copy, scale=gcol(g, c)))

            def h_final(d):
                if nstate["pending"] is not None and d > 0:
                    norm_chunk(d - 1, nstate["pending"])

            def rstd_sqrt(grp, bank, res):
                c0, cn = grp
                S.op("act", [res], [res], lambda: nc.scalar.activation(
                    out=bank[:, 0:cn], in_=bank[:, 0:cn], func=AF.Sqrt, bias=EPS, scale=256.0 / D))

            def rstd_recip(gi, grp, bank, res):
                c0, cn = grp
                S.op("dve", [res], [("rstdF", gi)], lambda: nc.vector.reciprocal(
                    out=rstdF[:, c0:c0 + cn], in_=bank[:, 0:cn]))

            def norm_finish():
                spec = nstate["active"]
                g, tts, with_xn = spec
                grps = stat_groups(tts)
                norm_chunk(NCH - 1, spec)
                if not with_xn:
                    k = 0
                    for ti, (t0, tn) in enumerate(tts):
                        for c in range(NCH):
                            if k % 2 == 0:
                                S.op("dve", [("h", c)], [("xn", c, ti)], lambda: nc.vector.tensor_scalar(
                                    out=xn[:, c, t0:t0 + tn], in0=h[:, c, t0:t0 + tn], scalar1=gcol(g, c), scalar2=None,
                                    op0=ALU.mult))
                            else:
                                S.op("act", [("h", c)], [("xn", c, ti)], lambda: nc.scalar.activation(
                                    out=xn[:, c, t0:t0 + tn], in_=h[:, c, t0:t0 + tn], func=AF.Copy, scale=gcol(g, c)))
                            k += 1
                rstd_sqrt(grps[0], SP0, "S0")
                rstd_sqrt(grps[1], SP1, "S1")
                rstd_recip(0, grps[0], SP0, "S0")
                rstd_recip(1, grps[1], SP1, "S1")
                c0, cn = grps[2]
                for c in range(NCH):
                    buf, br = (sqA, sqD)[c % 2][(c // 2) % 2], (("sqA", "sqD")[c % 2], (c // 2) % 2)
                    S.op("act", [("h", c)], [br], lambda: nc.scalar.activation(
                        out=buf[:, 0:cn], in_=h[:, c, c0:c0 + cn], func=AF.Square, scale=0.0625))
                    S.op("pe", [br], ["B0"], lambda: nc.tensor.matmul(
                        PB[0][:, 0:cn], lhsT=ones_h[:], rhs=buf[:, 0:cn], start=(c == 0), stop=(c == NCH - 1)))
                rstd_sqrt(grps[2], PB[0], "B0")
                rstd_recip(2, grps[2], PB[0], "B0")
                nstate["active"] = None

            def h_add(d, pset, pres, tts, scale):
                def emit():
                    last = None
                    for ti, (t0, tn) in enumerate(tts):
                        if scale is None:
                            last = nc.vector.tensor_tensor(out=h[:, d, t0:t0 + tn], in0=pset[ti][:, 0:tn],
                                                           in1=h[:, d, t0:t0 + tn], op=ALU.add)
                        else:
                            last = nc.vector.scalar_tensor_tensor(out=h[:, d, t0:t0 + tn], in0=pset[ti][:, 0:tn],
                                                                  scalar=scale, in1=h[:, d, t0:t0 + tn],
                                                                  op0=ALU.mult, op1=ALU.add)
                    return last
                S.op("dve", [pres, ("h", d)], [("h", d)], emit)

            def out_proj(name, tts):
                for d in range(NCH):
                    ws, wres = wget(name, 0, 0, 16, d * 128)
                    pset, pres = (PA, PA_R) if d % 2 == 0 else (PB, PB_R)
                    mm_group(ws, wres, 16, b2_rhs, tts, pset, pres, [("b2", k) for k in range(NCH)])
                    h_final(d)
                    h_add(d, pset, pres, tts, None)

            def ffn(nm, l, g, tts, hook=None):
                with contextlib.ExitStack() as fs:
                    norm_finish()
                    sg = [fs.enter_context(sbt("sg%d" % i, [128, TH], F32)) for i in range(2)]
                    ut = fs.enter_context(sbt("ut", [128, TH], F32))
                    for (f0, f1) in PIECES:
                        for f in range(f0, f1):
                            ws, wres = wget(nm + "_g", l, 0, 16, f * 128)
                            mm_group(ws, wres, 16, xn_rhs, tts, PA, PA_R, XN_ALL, tile_major=(f == 0))
                            sgb = sg[f % 2]

                            def emit_gs(sgb=sgb):
                                last = None
                                for ti, (t0, tn) in enumerate(tts):
                                    last = nc.vector.tensor_tensor(out=sgb[:, t0:t0 + tn], in0=PA[ti][:, 0:tn],
                                                                   in1=rstdF[:, t0:t0 + tn], op=ALU.mult)
                                return last
                            S.op("dve", [PA_R, RSTD_ALL], [("sg", f % 2)], emit_gs)

                            def emit_silu(sgb=sgb):
                                t0, t1 = tts[0][0], tts[2][0] + tts[2][1]
                                return nc.scalar.activation(out=sgb[:, t0:t1], in_=sgb[:, t0:t1], func=AF.Silu)
                            S.op("act", [("sg", f % 2)], [("sg", f % 2)], emit_silu)
                            ws, wres = wget(nm + "_u", l, 0, 16, f * 128)
                            mm_group(ws, wres, 16, xn_rhs, tts, PB, PB_R, XN_ALL)

                            def emit_us():
                                last = None
                                for ti, (t0, tn) in enumerate(tts):
                                    last = nc.vector.tensor_tensor(out=ut[:, t0:t0 + tn], in0=PB[ti][:, 0:tn],
                                                                   in1=rstdF[:, t0:t0 + tn], op=ALU.mult)
                                return last
                            S.op("dve", [PB_R, RSTD_ALL], ["ut"], emit_us)

                            def emit_hid(sgb=sgb, kk=f - f0):
                                t0, t1 = tts[0][0], tts[2][0] + tts[2][1]
                                return nc.vector.tensor_tensor(out=buf2[:, kk, t0:t1], in0=ut[:, t0:t1],
                                                               in1=sgb[:, t0:t1], op=ALU.mult)
                            S.op("dve", ["ut", ("sg", f % 2)], [("b2", f - f0)], emit_hid)
                            if hook is not None and f == 2:
                                hook(fs)
                        nkp = f1 - f0
                        for d in range(NCH):
                            ws, wres = wget(nm + "_d", l, f0, nkp, d * 128)
                            pset, pres = (PA, PA_R) if d % 2 == 0 else (PB, PB_R)
                            mm_group(ws, wres, nkp, b2_rhs, tts, pset, pres, [("b2", k) for k in range(nkp)], split_last=(d == 0))
                            if f1 == NF:
                                h_final(d)
                            h_add(d, pset, pres, tts, 0.5)
                    S.barrier()

            def load_p(l, pT, ptok):
                pblocks = [(j * 128, 128, j * 128) for j in range(9)] + [(1152, 16, 1152)]
                load_T("p%d" % l, lambda r0, nr: pin[l, r0:r0 + nr, :], pblocks, 2, (pT, "pT"), 0, DPLE, tok=ptok)

            def ple(l, tts, pT):
                with contextlib.ExitStack() as ps_:
                    norm_finish()
                    sig = [ps_.enter_context(sbt("sig%d" % i, [128, TH], F32)) for i in range(2)]
                    tmp = ps_.enter_context(sbt("pletmp", [128, TH], F32))
                    for d in range(NCH):
                        ws, wres = wget("ple_g", l, 0, 16, d * 128)
                        mm_group(ws, wres, 16, xn_rhs, tts, PA, PA_R, XN_ALL, tile_major=(d == 0))
                        h_final(d)
                        sb = sig[d % 2]

                        def emit_sgs(sb=sb):
                            last = None
                            for ti, (t0, tn) in enumerate(tts):
                                last = nc.vector.tensor_tensor(out=sb[:, t0:t0 + tn], in0=PA[ti][:, 0:tn],
                                                               in1=rstdF[:, t0:t0 + tn], op=ALU.mult)
                            return last
                        S.op("dve", [PA_R, RSTD_ALL], [("sig", d % 2)], emit_sgs)

                        def emit_sig(sb=sb):
                            t0, t1 = tts[0][0], tts[2][0] + tts[2][1]
                            return nc.scalar.activation(out=sb[:, t0:t1], in_=sb[:, t0:t1], func=AF.Sigmoid)
                        S.op("act", [("sig", d % 2)], [("sig", d % 2)], emit_sig)
                        ws, wres = wget("ple_p", l, 0, 2, d * 128)
                        mm_group(ws, wres, 2, lambda k, t0, tn: pT[:, k, t0:t0 + tn], tts, PB, PB_R,
                                 [("pT", 0), ("pT", 1)])

                        def emit_pp(sb=sb):
                            last = None
                            for ti, (t0, tn) in enumerate(tts):
                                last = nc.vector.tensor_tensor(out=tmp[:, t0:t0 + tn], in0=PB[ti][:, 0:tn],
                                                               in1=sb[:, t0:t0 + tn], op=ALU.mult)
                            return last
                        S.op("dve", [PB_R, ("sig", d % 2)], ["pletmp"], emit_pp)

                        def emit_hadd(d=d):
                            last = None
                            for ti, (t0, tn) in enumerate(tts):
                                last = nc.vector.tensor_tensor(out=h[:, d, t0:t0 + tn], in0=tmp[:, t0:t0 + tn],
                                                               in1=h[:, d, t0:t0 + tn], op=ALU.add)
                            return last
                        S.op("dve", ["pletmp", ("h", d)], [("h", d)], emit_hadd)
                    S.barrier()

            def gmlp_setup(wmT, bias2, ss):
                    if True:
                        wsn = ss.enter_context(sbt("wsn", [128, 8, 128], F32))
                        bsb = ss.enter_context(sbt("bsb", [128, 8, 128], F32))
                        pt = [v4(SP0), v4(SP1)]
                        S.dma("sp", [], ["wsn"], ("g", "ld", 0), lambda: nc.sync.dma_start(
                            out=wsn[:], in_=a_w_s[0].rearrange("h t s -> t h s")))
                        S.dma("sp", [], ["bsb"], ("g", "ld", 1), lambda: nc.sync.dma_start(
                            out=bsb[:].rearrange("p h t -> p (h t)"),
                            in_=bass.AP(a_b_s.tensor, 0, [[0, 128], [1, 8 * 128]])))
                        for hh in range(8):
                            pq = pt[hh % 2]
                            pr = "S%d" % (hh % 2)
                            S.op("pe", ["wsn"], [pr], lambda hh=hh, pq=pq: nc.tensor.transpose(
                                out=pq[:, 0, :], in_=wsn[:, hh, :], identity=ident_f[:]))
                            S.op("dve", [pr], [("wmT", hh)], lambda hh=hh, pq=pq: nc.vector.tensor_tensor(
                                out=wmT[:, hh, :], in0=pq[:, 0, :], in1=tri_f[:], op=ALU.mult))
                            S.op("pe", [("wmT", hh)], [pr], lambda hh=hh, pq=pq: nc.tensor.matmul(
                                pq[:, 1, :], lhsT=ones_b[:], rhs=wmT[:, hh, :], start=True, stop=True))
                            for c in (2 * hh, 2 * hh + 1):
                                S.op("dve", [pr, "bsb"], [("bias2", c)],
                                     lambda hh=hh, pq=pq, c=c: nc.vector.scalar_tensor_tensor(
                                         out=bias2[:, c, :], in0=pq[:, 1, :], scalar=gcol(G_LNB, c), in1=bsb[:, hh, :],
                                         op0=ALU.mult, op1=ALU.add))

            def gmlp(wmT, bias2):
                with contextlib.ExitStack() as gs:
                    G = gs.enter_context
                    norm_finish()
                    vsf = G(sbt("vsf", [128, NCH, 32], F32))
                    vnf = G(sbt("vnf", [128, NCH, 16], F32))
                    vhb = [G(sbt("vhb%d" % i, [128, D], BF16)) for i in range(2)]
                    st6 = [G(sbt("st6_%d" % i, [128, 4, 6], F32)) for i in range(2)]
                    mv = [G(sbt("mv%d" % i, [128, 2], F32)) for i in range(2)]
                    rt1 = [G(sbt("rt1_%d" % i, [128, 1], F32)) for i in range(2)]
                    rs1 = [G(sbt("rs1_%d" % i, [128, 1], F32)) for i in range(2)]
                    nmr = [G(sbt("nmr%d" % i, [128, 1], F32)) for i in range(2)]
                    ms_ = G(sbt("ms_", [128, 16], F32))
                    msq = G(sbt("msq", [128, 16], F32))
                    var = G(sbt("var", [128, 16], F32))
                    rts = G(sbt("rts", [128, 16], F32))
                    rss = G(sbt("rss", [128, 16], F32))
                    t1 = G(sbt("t1", [128, 16], F32))
                    vso = G(sbt("vso", [128, 1024], F32))
                    ug = [G(sbt("pm%d" % i, [128, TH], F32)) for i in range(2)]
                    for c in range(NCH):
                        ws, wres = wget("a_in", 0, 0, 16, D + c * 128)
                        pset, pres = (PA, PA_R) if c % 2 == 0 else (PB, PB_R)
                        mm_group(ws, wres, 16, xn_rhs, TT_FULL, pset, pres, XN_ALL, tile_major=(c == 0))

                        pmb = ug[c % 2]

                        def emit_vs(pmb=pmb, pset=pset):
                            last = None
                            for ti, (t0, tn) in enumerate(TT_FULL):
                                last = nc.vector.tensor_tensor(out=pmb[:, t0:t0 + tn], in0=pset[ti][:, 0:tn],
                                                               in1=rstdF[:, t0:t0 + tn], op=ALU.mult)
                            return last
                        S.op("dve", [pres, RSTD_ALL], [("ug", c % 2)], emit_vs)

                        def emit_v(c=c, pmb=pmb):
                            nc.scalar.activation(out=buf2[:, c, 0:C_SAMP], in_=pmb[:, 0:C_SAMP], func=AF.Gelu_apprx_tanh)
                            return nc.scalar.activation(out=vsf[:, c, 0:16], in_=pmb[:, C_SAMP:C_SAMP + 16],
                                                        func=AF.Gelu_apprx_tanh)
                        S.op("act", [("ug", c % 2)], [("b2", c), ("vsf", c)], emit_v)
                        S.op("act", [("vsf", c)], [("vsq", c)], lambda c=c: nc.scalar.activation(
                            out=vsf[:, c, 16:32], in_=vsf[:, c, 0:16], func=AF.Square))
                    pvb = [[SP0[:].bitcast(BF16), SP1[:].bitcast(BF16)], [PA[2][:].bitcast(BF16), PB[2][:].bitcast(BF16)]]
                    pvr = [["S0", "S1"], ["A2", "B2"]]
                    def stage_a1(j):
                        c0, jj = j * 128, j % 2
                        pv, pr = pvb[jj], pvr[jj]

                        def emit_tr():
                            last = None
                            for c in range(NCH):
                                last = nc.tensor.transpose(out=pv[c // 8][:, (c % 8) * 128:(c % 8 + 1) * 128],
                                                           in_=buf2[:, c, c0:c0 + 128], identity=ident_b[:])
                            return last
                        S.op("pe", [("b2", c) for c in range(NCH)], [pr], emit_tr)

                        def emit_stats():
                            last = None
                            for q in range(4):
                                last = nc.vector.bn_stats(out=st6[jj][:, q, :], in_=pv[q // 2][:, (q % 2) * 512:(q % 2 + 1) * 512])
                            return last
                        S.op("dve", [pr], [("st6", jj)], emit_stats)
                        S.op("dve", [("st6", jj)], [("mv", jj)], lambda: nc.vector.bn_aggr(
                            out=mv[jj][:], in_=st6[jj][:].rearrange("p a b -> p (a b)")))
                        S.op("act", [("mv", jj)], [("rt1", jj)], lambda: nc.scalar.activation(
                            out=rt1[jj][:], in_=mv[jj][:, 1:2], func=AF.Sqrt, bias=EPS, scale=1.0))

                    def stage_a2(j):
                        jj = j % 2
                        pv, pr = pvb[jj], pvr[jj]
                        S.op("dve", [("rt1", jj)], [("rs1", jj)], lambda: nc.vector.reciprocal(
                            out=rs1[jj][:], in_=rt1[jj][:]))
                        S.op("dve", [("rs1", jj), ("mv", jj)], [("nmr", jj)], lambda: nc.vector.scalar_tensor_tensor(
                            out=nmr[jj][:], in0=mv[jj][:, 0:1], scalar=-1.0, in1=rs1[jj][:], op0=ALU.mult, op1=ALU.mult))
                        vb = vhb[jj]

                        def emit_vh():
                            last = None
                            for q in range(4):
                                last = nc.scalar.activation(out=vb[:, q * 512:(q + 1) * 512],
                                                            in_=pv[q // 2][:, (q % 2) * 512:(q % 2 + 1) * 512],
                                                            func=AF.Identity, bias=nmr[jj][:, 0:1], scale=rs1[jj][:, 0:1])
                            return last
                        S.op("act", [pr, ("nmr", jj), ("rs1", jj)], [("vhb", jj)], emit_vh)

                    def stage_b(j):
                        c0, jj = j * 128, j % 2
                        vb = vhb[jj]
                        for gq in range(2):
                            pset = PA if gq == 0 else PB
                            mres = ["A0", "A1"] if gq == 0 else ["B0", "B1"]

                            def emit_mix(gq=gq, pset=pset):
                                last = None
                                for i in range(8):
                                    c = gq * 8 + i
                                    last = nc.tensor.matmul(pset[i // 4][:, (i % 4) * 128:(i % 4 + 1) * 128],
                                                            lhsT=vb[:, c * 128:(c + 1) * 128], rhs=wmT[:, c // 2, :],
                                                            start=True, stop=True)
                                return last
                            S.op("pe", [("vhb", jj)] + [("wmT", hh) for hh in range(8)], [mres], emit_mix)

                            def emit_sev(gq=gq, pset=pset):
                                last = None
                                for i in range(8):
                                    c = gq * 8 + i
                                    last = nc.vector.scalar_tensor_tensor(
                                        out=buf2[:, c, c0:c0 + 128], in0=pset[i // 4][:, (i % 4) * 128:(i % 4 + 1) * 128],
                                        scalar=gcol(G_LNG, c), in1=bias2[:, c, :], op0=ALU.mult, op1=ALU.add)
                                return last
                            S.op("dve", [mres] + [("bias2", gq * 8 + i) for i in range(8)],
                                 [("b2c", gq * 8 + i, j) for i in range(8)], emit_sev)

                    stage_a1(0)
                    stage_a2(0)
                    for j in range(9):
                        if j + 1 < 9:
                            stage_a1(j + 1)
                            stage_a2(j + 1)
                        stage_b(j)
                    S.op("dve", [("b2c", c, j) for c in range(NCH) for j in range(9)], [("b2", c) for c in range(NCH)] + ["t1"],
                         lambda: nc.vector.memset(t1[:, 0:1], 0.0))

                    def sample_path():
                        pss = SP0
                        pso = v4(SP1)

                        def emit_sst():
                            last = None
                            for c in range(NCH):
                                last = nc.tensor.matmul(pss[:, 0:32], lhsT=ones_f[:], rhs=vsf[:, c, :],
                                                        start=(c == 0), stop=(c == NCH - 1))
                            return last
                        S.op("pe", [("vsf", c) for c in range(NCH)] + [("vsq", c) for c in range(NCH)], ["S0"], emit_sst)
                        S.op("dve", ["S0"], ["ms_"], lambda: nc.vector.tensor_scalar(
                            out=ms_[:], in0=pss[:, 0:16], scalar1=1.0 / D, scalar2=None, op0=ALU.mult))
                        S.op("dve", ["ms_"], ["msq"], lambda: nc.vector.tensor_tensor(
                            out=msq[:], in0=ms_[:], in1=ms_[:], op=ALU.mult))
                        S.op("dve", ["S0", "msq"], ["var"], lambda: nc.vector.scalar_tensor_tensor(
                            out=var[:], in0=pss[:, 16:32], scalar=1.0 / D, in1=msq[:], op0=ALU.mult, op1=ALU.subtract))
                        S.op("act", ["var"], ["rts"], lambda: nc.scalar.activation(
                            out=rts[:], in_=var[:], func=AF.Sqrt, bias=EPS, scale=1.0))
                        S.op("dve", ["rts"], ["rss"], lambda: nc.vector.reciprocal(out=rss[:], in_=rts[:]))
                        for c in range(NCH):
                            S.op("dve", [("vsf", c), "ms_"], ["t1"], lambda c=c: nc.vector.tensor_tensor(
                                out=t1[:], in0=vsf[:, c, 0:16], in1=ms_[:], op=ALU.subtract))
                            S.op("dve", ["t1", "rss"], ["t1"], lambda: nc.vector.tensor_tensor(
                                out=t1[:], in0=t1[:], in1=rss[:], op=ALU.mult))
                            S.op("act", ["t1"], [("vnf", c)], lambda c=c: nc.scalar.activation(
                                out=vnf[:, c, :], in_=t1[:], func=AF.Identity, scale=gcol(G_LNG, c), bias=gcol(G_LNB, c)))
                            S.op("act", [("vnf", c)], [("b2s", c)], lambda c=c: nc.scalar.activation(
                                out=buf2[:, c, C_SAMP:C_SAMP + 16], in_=vnf[:, c, :], func=AF.Identity,
                                scale=w00b[:, c // 2, :], bias=b0b[:, c // 2, :]))
                    def sample_out():
                        pso = v4(SP1)
                        for half in range(2):
                            for q in range(2):
                                def emit_vt(half=half, q=q):
                                    last = None
                                    for cc in range(4):
                                        c = half * 8 + q * 4 + cc
                                        last = nc.tensor.transpose(out=pso[0:16, cc, :], in_=vnf[:, c, :], identity=ident_f[:])
                                    return last
                                S.op("pe", [("vnf", c) for c in range(NCH)], ["S1"], emit_vt)
                                S.op("act", ["S1"], ["vso"], lambda q=q: nc.scalar.activation(
                                    out=vso[0:16, q * 512:(q + 1) * 512], in_=pso[0:16, :, :].rearrange("p a b -> p (a b)"),
                                    func=AF.Copy))
                            S.dma("sp", ["vso"], [], ("o", "vs"), lambda half=half: nc.sync.dma_start(
                                out=vs_out[:, half * 1024:(half + 1) * 1024], in_=vso[0:16, :]))

                    sample_path()
                    for c in range(NCH):
                        ws, wres = wget("a_in", 0, 0, 16, c * 128)
                        pset, pres = (PA, PA_R) if c % 2 == 0 else (PB, PB_R)
                        mm_group(ws, wres, 16, xn_rhs, TT, pset, pres, XN_ALL)
                        if c == 5:
                            sample_out()
                        ub = ug[c % 2]

                        def emit_us(ub=ub, pset=pset):
                            last = None
                            for ti, (t0, tn) in enumerate(TT):
                                last = nc.vector.tensor_tensor(out=ub[:, t0:t0 + tn], in0=pset[ti][:, 0:tn],
                                                               in1=rstdF[:, t0:t0 + tn], op=ALU.mult)
                            return last
                        S.op("dve", [pres, RSTD_ALL], [("ug", c % 2)], emit_us)
                        S.op("act", [("ug", c % 2)], [("ug", c % 2)], lambda ub=ub: nc.scalar.activation(
                            out=ub[:, 126:TH], in_=ub[:, 126:TH], func=AF.Gelu_apprx_tanh))
                        S.op("dve", [("ug", c % 2), ("b2", c), ("b2s", c)], [("b2", c)], lambda ub=ub, c=c: nc.vector.tensor_tensor(
                            out=buf2[:, c, 126:TH], in0=ub[:, 126:TH], in1=buf2[:, c, 126:TH], op=ALU.mult))
                    out_proj("a_out", TT)
                    S.barrier()

            def load_state(stT, fs):
                stok = [fs.enter_context(sbt("stok%d" % i, [128, D], F32)) for i in range(2)]
                for r in range(2):
                    load_T("st%d" % r, lambda r0, nr, r=r: stc[:, r, :], [(0, 16, 0)], NCH, (stT, "stT"), r * 16, D,
                           tok=[stok[r], stok[r]])

            def sconv(stT):
                with contextlib.ExitStack() as cs:
                    C = cs.enter_context
                    norm_finish()
                    rstd2 = C(sbt("rstd2", [128, TH], F32))
                    S.op("dve", [RSTD_ALL], ["rstd2"], lambda: nc.vector.tensor_tensor(
                        out=rstd2[:, 126:TH], in0=rstdF[:, 126:TH], in1=rstdF[:, 126:TH], op=ALU.mult))
                    cgs = C(sbt("cgs", [128, TH], F32))
                    cib = [C(sbt("cib%d" % i, [128, 1026], F32)) for i in range(2)]
                    coe = C(sbt("coe", [128, 1042], F32))
                    cit = C(sbt("cit", [128, NCH, 18], F32))
                    for c in range(NCH):
                        w0, w1, w2 = gcol(G_WC, c), gcol(G_WC + 1, c), gcol(G_WC + 2, c)
                        if c % 2 == 0:
                            PX, pxr, PY, pyr = PA, PA_R, PB, PB_R
                        else:
                            PX, pxr, PY, pyr = PB, PB_R, PA, PA_R
                        ws, wres = wget("c_in", 0, 0, 16, D + c * 128)
                        mm_group(ws, wres, 16, xn_rhs, TT, PX, pxr, XN_ALL, tile_major=(c == 0))

                        def emit_cg(PX=PX):
                            last = None
                            for ti, (t0, tn) in enumerate(TT):
                                last = nc.vector.tensor_tensor(out=cgs[:, t0:t0 + tn], in0=PX[ti][:, 0:tn],
                                                               in1=rstd2[:, t0:t0 + tn], op=ALU.mult)
                            return last
                        S.op("dve", [pxr, "rstd2"], ["cgs"], emit_cg)
                        ws, wres = wget("c_in", 0, 0, 16, 2 * D + c * 128)
                        mm_group(ws, wres, 16, xn_rhs, TT, PY, pyr, XN_ALL)
                        cb = cib[c % 2]

                        def emit_ci(cb=cb, c=c, PY=PY):
                            last = None
                            for ti, (t0, tn) in enumerate(TT):
                                nm = min(t0 + tn, C_SAMP) - t0
                                last = nc.vector.tensor_tensor(out=cb[:, t0 - 126:t0 - 126 + nm], in0=PY[ti][:, 0:nm],
                                                               in1=cgs[:, t0:t0 + nm], op=ALU.mult)
                                if nm < tn:
                                    last = nc.vector.tensor_tensor(out=cit[:, c, 2:18], in0=PY[ti][:, nm:tn],
                                                                   in1=cgs[:, C_SAMP:C_SAMP + 16], op=ALU.mult)
                            return last
                        S.op("dve", [pyr, "cgs"], [("cib", c % 2), ("cit", c)], emit_ci)

                        def emit_cp(cb=cb, c=c):
                            nc.vector.tensor_copy(out=cit[:, c, 0:2], in_=cb[:, 1024:1026])
                            nc.vector.tensor_copy(out=coe[:, 0:2], in_=cb[:, 0:2])
                            nc.vector.tensor_scalar(out=coe[:, 1026:1042], in0=stT[:, c, :], scalar1=w0, scalar2=None,
                                                    op0=ALU.mult)
                            return nc.vector.tensor_scalar(out=coe[:, 2:1026], in0=cb[:, 0:1024], scalar1=w0, scalar2=None,
                                                           op0=ALU.mult)
                        S.op("dve", [("cib", c % 2)] + [("stT", k) for k in range(32)], ["coe", ("cit", c)], emit_cp)

                        def emit_c1(cb=cb, c=c):
                            nc.vector.scalar_tensor_tensor(out=coe[:, 1026:1042], in0=stT[:, 16 + c, :], scalar=w1,
                                                           in1=coe[:, 1026:1042], op0=ALU.mult, op1=ALU.add)
                            return nc.vector.scalar_tensor_tensor(out=coe[:, 2:1026], in0=cb[:, 1:1025], scalar=w1,
                                                                  in1=coe[:, 2:1026], op0=ALU.mult, op1=ALU.add)
                        S.op("dve", [("cib", c % 2), "coe"], ["coe"], emit_c1)

                        def emit_c2(cb=cb, c=c):
                            nc.vector.scalar_tensor_tensor(out=coe[:, 1026:1042], in0=cit[:, c, 2:18], scalar=w2,
                                                           in1=coe[:, 1026:1042], op0=ALU.mult, op1=ALU.add)
                            return nc.vector.scalar_tensor_tensor(out=coe[:, 2:1026], in0=cb[:, 2:1026], scalar=w2,
                                                                  in1=coe[:, 2:1026], op0=ALU.mult, op1=ALU.add)
                        S.op("dve", [("cib", c % 2), ("cit", c), "coe"], ["coe"], emit_c2)
                        S.op("dve", ["coe", RSTD_ALL], ["coe"], lambda: nc.vector.tensor_tensor(
                            out=coe[:, 0:1042], in0=coe[:, 0:1042], in1=rstdF[:, 126:TH], op=ALU.mult))
                        ws, wres = wget("c_in", 0, 0, 16, c * 128)
                        mm_group(ws, wres, 16, xn_rhs, TT, PX, pxr, XN_ALL)

                        def emit_yc(c=c, PX=PX):
                            last = None
                            for ti, (t0, tn) in enumerate(TT):
                                last = nc.vector.tensor_tensor(out=buf2[:, c, t0:t0 + tn], in0=PX[ti][:, 0:tn],
                                                               in1=coe[:, t0 - 126:t0 - 126 + tn], op=ALU.mult)
                            return last
                        S.op("dve", [pxr, "coe"], [("b2", c)], emit_yc)
                    with contextlib.ExitStack() as c2:
                        pso = v4(SP1)
                        cio = c2.enter_context(sbt("cio", [128, 1024], F32))
                        S.dma("sp", [], [], ("o", "ncs0"), lambda: nc.sync.dma_start(out=ncs_out[:, 0, :], in_=stc[:, 1, :]))
                        for half in range(2):
                            for q in range(2):
                                def emit_ct(half=half, q=q):
                                    last = None
                                    for cc in range(4):
                                        c = half * 8 + q * 4 + cc
                                        last = nc.tensor.transpose(out=pso[0:18, cc, :], in_=cit[:, c, :], identity=ident_f[:])
                                    return last
                                S.op("pe", [("cit", c) for c in range(NCH)], ["S1"], emit_ct)
                                S.op("act", ["S1"], ["cio"], lambda q=q: nc.scalar.activation(
                                    out=cio[0:18, q * 512:(q + 1) * 512], in_=pso[0:18, :, :].rearrange("p a b -> p (a b)"),
                                    func=AF.Copy))
                            S.dma("sp", ["cio"], [], ("o", "ncp"), lambda half=half: nc.sync.dma_start(
                                out=ncp_out[:, half * 1024:(half + 1) * 1024], in_=cio[0:2, :]))
                            S.dma("sp", ["cio"], [], ("o", "ncs1"), lambda half=half: nc.sync.dma_start(
                                out=ncs_out[:, 1, half * 1024:(half + 1) * 1024], in_=cio[2:18, :]))
                        S.barrier()
                    out_proj("c_out", TT)
                    S.barrier()

            def layer_tail(l, after_spec):
                with contextlib.ExitStack() as ts:
                    pT = ts.enter_context(sbt("pT", [128, 2, TH], BF16))
                    ptok = [ts.enter_context(sbt("ptok%d" % i, [128, DPLE], F32)) for i in range(2)]
                    run_phase(lambda: ffn("ffn2", l, G_FFN2, TT, hook=lambda fs: load_p(l, pT, ptok)),
                              (G_PLE + l, TT, True))
                    run_phase(lambda: ple(l, TT, pT), after_spec)
                    S.barrier()

            def run_phase(fn, next_spec):
                nstate["pending"] = next_spec
                fn()
                nstate["active"] = next_spec
                nstate["pending"] = None

            first = (G_FFN1 + 0, TT_FULL, True)
            nstate["active"] = first
            for c in range(NCH - 1):
                norm_chunk(c, first, xn_dve=(1, 2))
            with contextlib.ExitStack() as g0:
                wmT = g0.enter_context(sbt("wmT", [128, 8, 128], BF16))
                bias2 = g0.enter_context(sbt("bias2", [128, NCH, 128], F32))
                run_phase(lambda: ffn("ffn1", 0, G_FFN1, TT_FULL, hook=lambda fs: gmlp_setup(wmT, bias2, fs)),
                          (G_MIX + 0, TT_FULL, True))
                run_phase(lambda: gmlp(wmT, bias2), (G_FFN2 + 0, TT, True))
                S.barrier()
            layer_tail(0, (G_FFN1 + 1, TT, False))
            with contextlib.ExitStack() as c0s:
                stT = c0s.enter_context(sbt("stT", [128, 32, 16], F32))
                run_phase(lambda: ffn("ffn1", 1, G_FFN1, TT, hook=lambda fs: load_state(stT, fs)), (G_MIX + 1, TT, True))
                run_phase(lambda: sconv(stT), (G_FFN2 + 1, TT, True))
                S.barrier()
            layer_tail(1, None)
            assert wstate["used"] == len(plan)
            S.barrier()

        with contextlib.ExitStack() as fs:
            Fz = fs.enter_context
            gfb = Fz(sbt("gfb", [128, D], F32))
            yo = [Fz(sbt("yo%d" % i, [128, D], F32)) for i in range(2)]
            pf = [banks[0:4], banks[4:8]]
            st6 = Fz(sbt("fst6", [128, 4, 6], F32))
            mv = Fz(sbt("fmv", [128, 2], F32))
            ms2 = Fz(sbt("fms2", [128, 1], F32))
            rt = Fz(sbt("frt", [128, 1], F32))
            rs = Fz(sbt("frs", [128, 1], F32))
            S.dma("sp", [], ["gfb"], ("f", "g"), lambda: nc.sync.dma_start(
                out=gfb[:], in_=bass.AP(final_norm.tensor, 0, [[0, 128], [1, D]])))
            fblocks = [(C_MAIN + j * 128, 128, j * 128, 0) for j in range(8)] + [(C_SAMP, 16, 1024, 0)]
            for bi, (c0, n, row0, skip) in enumerate(fblocks):
                pfb = pf[bi % 2]
                yb = yo[bi % 2]

                def emit_ft(pfb=pfb, c0=c0, n=n):
                    last = None
                    for c in range(NCH):
                        last = nc.tensor.transpose(out=pfb[c // 4][0:n, (c % 4) * 128:(c % 4 + 1) * 128],
                                                   in_=h[:, c, c0:c0 + n], identity=ident_f[:])
                    return last
                S.op("pe", [("h", c) for c in range(NCH)], [("pf", bi % 2)], emit_ft)

                def emit_fs(pfb=pfb, n=n):
                    last = None
                    for q in range(4):
                        last = nc.vector.bn_stats(out=st6[0:n, q, :], in_=pfb[q][0:n, :])
                    return last
                S.op("dve", [("pf", bi % 2)], ["fst6"], emit_fs)
                S.op("dve", ["fst6"], ["fmv"], lambda n=n: nc.vector.bn_aggr(
                    out=mv[0:n, :], in_=st6[0:n, :, :].rearrange("p a b -> p (a b)")))
                S.op("dve", ["fmv"], ["fms2"], lambda n=n: nc.vector.scalar_tensor_tensor(
                    out=ms2[0:n, :], in0=mv[0:n, 0:1], scalar=mv[0:n, 0:1], in1=mv[0:n, 1:2], op0=ALU.mult, op1=ALU.add))
                S.op("act", ["fms2"], ["frt"], lambda n=n: nc.scalar.activation(
                    out=rt[0:n, :], in_=ms2[0:n, :], func=AF.Sqrt, bias=EPS, scale=1.0))
                S.op("dve", ["frt"], ["frs"], lambda n=n: nc.vector.reciprocal(out=rs[0:n, :], in_=rt[0:n, :]))

                def emit_fo(pfb=pfb, yb=yb, n=n):
                    last = None
                    for q in range(4):
                        last = nc.vector.scalar_tensor_tensor(out=yb[0:n, q * 512:(q + 1) * 512],
                                                              in0=pfb[q][0:n, :], scalar=rs[0:n, 0:1],
                                                              in1=gfb[0:n, q * 512:(q + 1) * 512], op0=ALU.mult, op1=ALU.mult)
                    return last
                S.op("dve", [("pf", bi % 2), "frs", "gfb"], [("yo", bi % 2)], emit_fo)
                S.dma("sp", [("yo", bi % 2)], [], ("o", "y", bi % 2),
                      lambda yb=yb, n=n, row0=row0, skip=skip: nc.sync.dma_start(
                          out=y_out[row0:row0 + n - skip, :], in_=yb[skip:n, :]))
            S.finish()
    return nc


_CACHE = {}


def _get_program():
    if "nc" not in _CACHE:
        _CACHE["nc"] = build_program()
    return _CACHE["nc"]


def kernel(x_prompt, x_sample, state_conv, p_prompt, p_sample,
           ffn1_norm, ffn1_w_gate, ffn1_w_up, ffn1_w_down,
           mix_norm, a_w_in, a_ln_g, a_ln_b, a_w_s, a_b_s, a_w_out,
           c_w_in, c_w_conv, c_w_out,
           ffn2_norm, ffn2_w_gate, ffn2_w_up, ffn2_w_down,
           ple_norm, ple_w_gate, ple_w_proj, final_norm):
    f = lambda a: np.ascontiguousarray(np.asarray(a, dtype=np.float32))
    x_prompt, x_sample, state_conv, p_prompt, p_sample = map(f, (x_prompt, x_sample, state_conv, p_prompt, p_sample))

    def colform(v):
        return f(v).reshape(16, 128).T

    groups = [None] * NGRP
    for l in range(2):
        groups[G_FFN1 + l] = colform(ffn1_norm[l])
        groups[G_MIX + l] = colform(mix_norm[l])
        groups[G_FFN2 + l] = colform(ffn2_norm[l])
        groups[G_PLE + l] = colform(ple_norm[l])
    groups[G_LNG] = colform(a_ln_g[0])
    groups[G_LNB] = colform(a_ln_b[0])
    for k in range(3):
        groups[G_WC + k] = colform(c_w_conv[0, k])
    cols = np.ascontiguousarray(np.concatenate(groups, axis=1))
    ident = np.eye(128, dtype=np.float32)
    tri = np.triu(np.ones((128, 128), dtype=np.float32))

    shared = {
        "cols": cols, "ident": ident, "tri": tri,
        "ffn1_w_gate": f(ffn1_w_gate), "ffn1_w_up": f(ffn1_w_up), "ffn1_w_down": f(ffn1_w_down),
        "ffn2_w_gate": f(ffn2_w_gate), "ffn2_w_up": f(ffn2_w_up), "ffn2_w_down": f(ffn2_w_down),
        "a_w_in": f(a_w_in), "a_w_out": f(a_w_out), "c_w_in": f(c_w_in), "c_w_out": f(c_w_out),
        "ple_w_gate": f(ple_w_gate), "ple_w_proj": f(ple_w_proj),
        "a_w_s": f(a_w_s), "a_b_s": f(a_b_s), "final_norm": f(final_norm),
    }
    in_maps = []
    for c in range(N_CORES):
        b, half = c // 2, c % 2
        xin = np.zeros((TH, D), np.float32)
        pin = np.zeros((2, TH, DPLE), np.float32)
        xin[128:1152] = x_prompt[b, half * 1024:(half + 1) * 1024]
        pin[:, 128:1152] = p_prompt[:, b, half * 1024:(half + 1) * 1024]
        if half == 1:
            xin[0:128] = x_prompt[b, 896:1024]
            pin[:, 0:128] = p_prompt[:, b, 896:1024]
        xin[1152:1168] = x_sample[c * 16:(c + 1) * 16, 0]
        pin[:, 1152:1168] = p_sample[:, c * 16:(c + 1) * 16, 0]
        m = dict(shared)
        m["xin"] = xin
        m["pin"] = pin
        m["stc"] = np.ascontiguousarray(state_conv[0, c * 16:(c + 1) * 16])
        in_maps.append(m)

    nc = _get_program()
    res = run_bass_kernel_spmd(nc, in_maps, core_ids=list(range(N_CORES)))
    r = res.results
    y_prompt = np.zeros((4, 2048, D), np.float32)
    y_sample = np.zeros((128, 1, D), np.float32)
    ncp = np.zeros((1, 4, 2, D), np.float32)
    ncs = np.zeros((1, 128, 2, D), np.float32)
    vs = np.zeros((1, 128, 1, D), np.float32)
    for c in range(N_CORES):
        b, half = c // 2, c % 2
        y_prompt[b, half * 1024:(half + 1) * 1024] = r[c]["y"][0:1024]
        y_sample[c * 16:(c + 1) * 16, 0] = r[c]["y"][1024:1040]
        if half == 1:
            ncp[0, b] = r[c]["ncp"]
        ncs[0, c * 16:(c + 1) * 16] = r[c]["ncs"]
        vs[0, c * 16:(c + 1) * 16, 0] = r[c]["vs"]
    return (y_prompt, y_sample, ncp, ncs, vs)
```

```python
import contextlib
import numpy as np
import concourse.bass as bass
import concourse.mybir as mybir
from concourse.bass_utils import run_bass_kernel_spmd

F32 = mybir.dt.float32
F32R = mybir.dt.float32r
BF16 = mybir.dt.bfloat16
F16 = mybir.dt.float16
AF = mybir.ActivationFunctionType
ALU = mybir.AluOpType

D = 2048
DFF = 5632
NCH = 16
NF = 44
DPLE = 256
TH = 1168
NMAIN = 1024
NSAMP = 16
TT_FULL = [(0, 390), (390, 390), (780, 388)]
TT = [(126, 348), (474, 348), (822, 346)]
C_MAIN = 128
C_SAMP = 1152
PIECES = [(0, 15), (15, 30), (30, 44)]
NSLOT = 3
EPS = 1e-6
N_CORES = 8

G_FFN1, G_MIX, G_FFN2, G_PLE = 0, 2, 4, 6
G_LNG, G_LNB, G_WC = 8, 9, 10
NGRP = 13


class Sync:
    def __init__(self, nc, es):
        self.nc = nc
        self.es = es
        self.engs = {"pe": nc.tensor, "act": nc.scalar, "dve": nc.vector, "pool": nc.gpsimd, "sp": nc.sync}
        self.sems = {}
        self.cnt = {}
        for e in ("pe", "act", "dve"):
            self.sems[e] = es.enter_context(nc.semaphore("s_" + e))
            self.cnt[e] = 0
        self.known = {e: {} for e in self.engs}
        self.last_w = {}
        self.readers = {}
        self.pending = {e: {} for e in self.engs}
        self.dma_events = {}

    def _sem(self, key):
        if key not in self.sems:
            self.sems[key] = self.es.enter_context(self.nc.semaphore("s_" + "_".join(str(k) for k in key)))
            self.cnt[key] = 0
        return self.sems[key]

    def _waits(self, eng, reads, writes):
        deps = dict(self.pending[eng])
        self.pending[eng] = {}

        def add(ev):
            k, v = ev
            if deps.get(k, 0) < v:
                deps[k] = v

        for r in reads:
            if r in self.last_w:
                add(self.last_w[r])
        for w in writes:
            if w in self.last_w:
                add(self.last_w[w])
            for ev in self.readers.get(w, {}).items():
                add(ev)
        kn = self.known[eng]
        for k, v in deps.items():
            if k == "pe" and eng == "pe":
                continue
            if kn.get(k, 0) >= v:
                continue
            self.engs[eng].wait_ge(self.sems[k], v)
            kn[k] = v

    def _record(self, ev, reads, writes):
        for r in reads:
            d = self.readers.setdefault(r, {})
            if d.get(ev[0], 0) < ev[1]:
                d[ev[0]] = ev[1]
        for w in writes:
            self.last_w[w] = ev
            self.readers[w] = {}

    @staticmethod
    def _flat(lst):
        out = []
        for r in lst:
            if isinstance(r, list):
                out.extend(r)
            else:
                out.append(r)
        return out

    def op(self, eng, reads, writes, emit):
        reads, writes = self._flat(reads), self._flat(writes)
        self._waits(eng, reads, writes)
        inst = emit()
        self.cnt[eng] += 1
        inst.then_inc(self.sems[eng], 1)
        self._record((eng, self.cnt[eng]), reads, writes)

    def dma(self, queue, reads, writes, semkey, emit):
        reads, writes = self._flat(reads), self._flat(writes)
        self._sem(semkey)
        self._waits(queue, reads, writes)
        inst = emit()
        self.cnt[semkey] += 16
        inst.then_inc(self.sems[semkey], 16)
        ev = (semkey, self.cnt[semkey])
        self._record(ev, reads, writes)
        if semkey[0] != "wld":
            self.dma_events[semkey] = self.cnt[semkey]

    def barrier(self, engs=("pe", "act", "dve", "sp")):
        snap = {e: self.cnt[e] for e in ("pe", "act", "dve") if self.cnt[e] > 0}
        snap.update(self.dma_events)
        for e in engs:
            p = self.pending[e]
            for k, v in snap.items():
                if k == e:
                    continue
                if p.get(k, 0) < v:
                    p[k] = v

    def finish(self):
        self.barrier()
        self._waits("sp", [], [])


def build_program():
    nc = bass.Bass("TRN2", target_bir_lowering=False)

    def din(name, shape):
        return nc.dram_tensor(name, list(shape), F32, kind="ExternalInput").ap()

    def dout(name, shape):
        return nc.dram_tensor(name, list(shape), F32, kind="ExternalOutput").ap()

    xin = din("xin", (TH, D))
    pin = din("pin", (2, TH, DPLE))
    stc = din("stc", (NSAMP, 2, D))
    cols = din("cols", (128, NGRP * 16))
    identd = din("ident", (128, 128))
    trid = din("tri", (128, 128))
    wd = {}
    for nm in ("ffn1", "ffn2"):
        wd[nm + "_g"] = din(nm + "_w_gate", (2, D, DFF))
        wd[nm + "_u"] = din(nm + "_w_up", (2, D, DFF))
        wd[nm + "_d"] = din(nm + "_w_down", (2, DFF, D))
    wd["a_in"] = din("a_w_in", (1, D, 2 * D))
    wd["a_out"] = din("a_w_out", (1, D, D))
    wd["c_in"] = din("c_w_in", (1, D, 3 * D))
    wd["c_out"] = din("c_w_out", (1, D, D))
    wd["ple_g"] = din("ple_w_gate", (2, D, D))
    wd["ple_p"] = din("ple_w_proj", (2, DPLE, D))
    a_w_s = din("a_w_s", (1, 8, 128, 128))
    a_b_s = din("a_b_s", (1, 8, 128))
    final_norm = din("final_norm", (D,))

    y_out = dout("y", (NMAIN + NSAMP, D))
    ncp_out = dout("ncp", (2, D))
    ncs_out = dout("ncs", (NSAMP, 2, D))
    vs_out = dout("vs", (NSAMP, D))

    plan = []

    def P(name, l, k0, nk, c0):
        plan.append((name, l, k0, nk, c0))

    def plan_ffn(nm, l):
        for (f0, f1) in PIECES:
            for f in range(f0, f1):
                P(nm + "_g", l, 0, 16, f * 128)
                P(nm + "_u", l, 0, 16, f * 128)
            for d in range(NCH):
                P(nm + "_d", l, f0, f1 - f0, d * 128)

    def plan_ple(l):
        for d in range(NCH):
            P("ple_g", l, 0, 16, d * 128)
            P("ple_p", l, 0, 2, d * 128)

    plan_ffn("ffn1", 0)
    for c in range(NCH):
        P("a_in", 0, 0, 16, D + c * 128)
    for c in range(NCH):
        P("a_in", 0, 0, 16, c * 128)
    for d in range(NCH):
        P("a_out", 0, 0, 16, d * 128)
    plan_ffn("ffn2", 0)
    plan_ple(0)
    plan_ffn("ffn1", 1)
    for c in range(NCH):
        P("c_in", 0, 0, 16, D + c * 128)
        P("c_in", 0, 0, 16, 2 * D + c * 128)
        P("c_in", 0, 0, 16, c * 128)
    for d in range(NCH):
        P("c_out", 0, 0, 16, d * 128)
    plan_ffn("ffn2", 1)
    plan_ple(1)

    uid = [0]

    def sbt(name, shape, dt):
        uid[0] += 1
        return nc.sbuf_tensor("%s_u%d" % (name, uid[0]), shape, dt)

    def pmt(name, shape, dt):
        uid[0] += 1
        return nc.psum_tensor("%s_u%d" % (name, uid[0]), shape, dt)

    with contextlib.ExitStack() as es:
        S = Sync(nc, es)
        E = es.enter_context

        h = E(sbt("h", [128, NCH, TH], F32))
        colt = E(sbt("colt", [128, NGRP * 16], F32))
        ident_f = E(sbt("ident_f", [128, 128], F32))
        ident_b = E(sbt("ident_b", [128, 128], BF16))
        tri_f = E(sbt("tri_f", [128, 128], F32))
        ones_f = E(sbt("ones_f", [128, 128], F32))
        ones_b = E(sbt("ones_b", [128, 128], BF16))
        ones_h = E(sbt("ones_h", [128, 128], F16))
        w00b = E(sbt("w00b", [128, 8, 1], F32))
        b0b = E(sbt("b0b", [128, 8, 1], F32))

        banks = [E(pmt("bank%d" % i, [128, 512], F32)) for i in range(8)]
        PA, PB, SP0, SP1 = banks[0:3], banks[3:6], banks[6], banks[7]
        PA_R, PB_R = ["A0", "A1", "A2"], ["B0", "B1", "B2"]

        def v4(bank):
            return bank[:].rearrange("p (a b) -> p a b", a=4)

        def gcol(g, c):
            return colt[:, g * 16 + c:g * 16 + c + 1]

        S.dma("sp", [], ["c_colt"], ("cst", 0), lambda: nc.sync.dma_start(out=colt[:], in_=cols))
        S.dma("sp", [], ["c_identf"], ("cst", 1), lambda: nc.sync.dma_start(out=ident_f[:], in_=identd))
        S.dma("sp", [], ["c_tri"], ("cst", 2), lambda: nc.sync.dma_start(out=tri_f[:], in_=trid))
        S.dma("sp", [], ["c_w00"], ("cst", 3), lambda: nc.sync.dma_start(
            out=w00b[:], in_=bass.AP(a_w_s.tensor, 0, [[0, 128], [128 * 128, 8], [1, 1]]),
            allow_slow_non_contiguous=True))
        S.dma("sp", [], ["c_b0"], ("cst", 4), lambda: nc.sync.dma_start(
            out=b0b[:], in_=bass.AP(a_b_s.tensor, 0, [[0, 128], [128, 8], [1, 1]]),
            allow_slow_non_contiguous=True))
        S.op("dve", [], ["c_onesf"], lambda: nc.vector.memset(ones_f[:], 1.0))
        S.op("dve", [], ["c_onesb"], lambda: nc.vector.memset(ones_b[:], 1.0))
        S.op("dve", [], ["c_onesh"], lambda: nc.vector.memset(ones_h[:], 1.0))
        S.op("dve", ["c_identf"], ["c_identb"], lambda: nc.vector.tensor_copy(out=ident_b[:], in_=ident_f[:]))
        S.barrier(engs=("pe", "act", "dve"))

        def load_T(tag, src_fn, blocks, nch, dest, dest_c0, width, tok=None, nbank=2, ntok=2):
            with contextlib.ExitStack() as ls:
                own = tok is None
                if own:
                    tok = [ls.enter_context(sbt("%s_tok%d" % (tag, i), [128, width], F32)) for i in range(ntok)]
                bank_ids = [6, 7, 0, 1, 2, 3, 4, 5][:nbank]
                bank_res = {6: "S0", 7: "S1", 0: "A0", 1: "A1", 2: "A2", 3: "B0", 4: "B1", 5: "B2"}
                pt = [v4(banks[b]) for b in bank_ids]
                ptr = [bank_res[b] for b in bank_ids]
                qi = 0
                for bi, (r0, nr, c0) in enumerate(blocks):
                    tb = tok[bi % len(tok)]
                    S.dma("sp", [], [(tag, "tok", bi % len(tok))], (tag, "ld", bi % len(tok)),
                          lambda tb=tb, r0=r0, nr=nr: nc.sync.dma_start(out=tb[0:nr, 0:nch * 128], in_=src_fn(r0, nr)))
                    for q in range((nch + 3) // 4):
                        ncc = min(4, nch - q * 4)
                        pq = pt[qi % nbank]
                        pqr = ptr[qi % nbank]

                        def emit_t(tb=tb, pq=pq, q=q, ncc=ncc, nr=nr):
                            last = None
                            for cc in range(ncc):
                                c = q * 4 + cc
                                last = nc.tensor.transpose(out=pq[:, cc, 0:nr], in_=tb[0:nr, c * 128:(c + 1) * 128],
                                                           identity=ident_f[0:nr, 0:nr])
                            return last

                        S.op("pe", [(tag, "tok", bi % len(tok))], [pqr], emit_t)
                        dres = [(dest[1], dest_c0 + q * 4 + cc) for cc in range(ncc)]
                        dap = dest[0][:, dest_c0 + q * 4:dest_c0 + q * 4 + ncc, c0:c0 + nr]
                        if qi % 2 == 0:
                            S.op("act", [pqr], dres,
                                 lambda dap=dap, pq=pq, ncc=ncc, nr=nr: nc.scalar.activation(
                                     out=dap, in_=pq[:, 0:ncc, 0:nr], func=AF.Copy))
                        else:
                            S.op("dve", [pqr], dres,
                                 lambda dap=dap, pq=pq, ncc=ncc, nr=nr: nc.vector.tensor_copy(
                                     out=dap, in_=pq[:, 0:ncc, 0:nr]))
                        qi += 1
                if own:
                    S.barrier()

        xblocks = [(j * 128, 128, j * 128) for j in range(9)] + [(1152, 16, 1152)]
        load_T("x", lambda r0, nr: xin[r0:r0 + nr, :], xblocks, NCH, (h, "h"), 0, D, nbank=8, ntok=4)

        with contextlib.ExitStack() as ms:
            M = ms.enter_context
            xn = M(sbt("xn", [128, NCH, TH], BF16))
            buf2 = M(sbt("buf2", [128, NCH, TH], BF16))
            wslots = [M(sbt("wsl%d" % i, [128, 16, 128], BF16)) for i in range(NSLOT)]

            XN_ALL = [("xn", c, ti) for c in range(NCH) for ti in range(3)]

            wstate = {"issued": 0, "used": 0}

            def wsrc(ent):
                name, l, k0, nk, c0 = ent
                return wd[name][l].rearrange("(k p) c -> p k c", p=128)[:, k0:k0 + nk, c0:c0 + 128]

            def wpump():
                while wstate["issued"] < min(len(plan), wstate["used"] + NSLOT):
                    i = wstate["issued"]
                    s = i % NSLOT
                    ent = plan[i]
                    nk = ent[3]
                    S.dma("pool", [], [("w", s)], ("wld", s),
                          lambda s=s, nk=nk, ent=ent: nc.gpsimd.dma_start(out=wslots[s][:, 0:nk, :], in_=wsrc(ent)))
                    wstate["issued"] += 1

            def wget(name, l, k0, nk, c0):
                i = wstate["used"]
                assert plan[i] == (name, l, k0, nk, c0), (i, plan[i], (name, l, k0, nk, c0))
                wpump()
                wstate["used"] += 1
                s = i % NSLOT
                return wslots[s], ("w", s)

            def mm_group(ws, wres, nk, rhs_fn, tts, pset, pres, reads, tile_major=False, split_last=False):
                if split_last:
                    def emit_a():
                        last = None
                        for k in range(nk - 1):
                            for ti, (t0, tn) in enumerate(tts):
                                last = nc.tensor.matmul(pset[ti][:, 0:tn], lhsT=ws[:, k, :], rhs=rhs_fn(k, t0, tn),
                                                        start=(k == 0), stop=False)
                        return last
                    S.op("pe", [wres] + reads[:-1], [pres], emit_a)

                    def emit_b():
                        last = None
                        for ti, (t0, tn) in enumerate(tts):
                            last = nc.tensor.matmul(pset[ti][:, 0:tn], lhsT=ws[:, nk - 1, :], rhs=rhs_fn(nk - 1, t0, tn),
                                                    start=False, stop=True)
                        return last
                    S.op("pe", [wres, reads[-1]], [pres], emit_b)
                    return
                if tile_major:
                    for ti, (t0, tn) in enumerate(tts):
                        def emit_tm(ti=ti, t0=t0, tn=tn):
                            last = None
                            for k in range(nk):
                                last = nc.tensor.matmul(pset[ti][:, 0:tn], lhsT=ws[:, k, :], rhs=rhs_fn(k, t0, tn),
                                                        start=(k == 0), stop=(k == nk - 1))
                            return last
                        S.op("pe", [wres] + [("xn", c, ti) for c in range(NCH)], [pres[ti]], emit_tm)
                    return

                def emit():
                    last = None
                    for k in range(nk):
                        for ti, (t0, tn) in enumerate(tts):
                            last = nc.tensor.matmul(pset[ti][:, 0:tn], lhsT=ws[:, k, :], rhs=rhs_fn(k, t0, tn),
                                                    start=(k == 0), stop=(k == nk - 1))
                    return last
                S.op("pe", [wres] + reads, [pres], emit)

            def xn_rhs(k, t0, tn):
                return xn[:, k, t0:t0 + tn]

            def b2_rhs(k, t0, tn):
                return buf2[:, k, t0:t0 + tn]

            sqA = [M(sbt("sqA%d" % i, [128, 512], F16)) for i in range(2)]
            sqD = [M(sbt("sqD%d" % i, [128, 512], F16)) for i in range(2)]
            rstdF = M(sbt("rstdF", [128, TH], F32))
            RSTD_ALL = [("rstdF", gi) for gi in range(3)]
            nstate = {"active": None, "pending": None}

            def stat_groups(tts):
                lo = tts[0][0]
                hi = tts[2][0] + tts[2][1]
                return [(lo, 512), (lo + 512, 512), (lo + 1024, hi - lo - 1024)]

            def norm_chunk(c, spec, xn_dve=()):
                g, tts, with_xn = spec
                (a0, an), (b0, bn), _ = stat_groups(tts)
                S.op("act", [("h", c)], [("sqA", c % 2)], lambda: nc.scalar.activation(
                    out=sqA[c % 2][:, 0:an], in_=h[:, c, a0:a0 + an], func=AF.Square, scale=0.0625))
                S.op("pe", [("sqA", c % 2)], ["S0"], lambda: nc.tensor.matmul(
                    SP0[:, 0:an], lhsT=ones_h[:], rhs=sqA[c % 2][:, 0:an], start=(c == 0), stop=(c == NCH - 1)))
                S.op("dve", [("h", c)], [("sqD", c % 2)], lambda: nc.vector.scalar_tensor_tensor(
                    out=sqD[c % 2][:, 0:bn], in0=h[:, c, b0:b0 + bn], scalar=1.0 / 256, in1=h[:, c, b0:b0 + bn],
                    op0=ALU.mult, op1=ALU.mult))
                S.op("pe", [("sqD", c % 2)], ["S1"], lambda: nc.tensor.matmul(
                    SP1[:, 0:bn], lhsT=ones_h[:], rhs=sqD[c % 2][:, 0:bn], start=(c == 0), stop=(c == NCH - 1)))
                if with_xn:
                    for ti, (t0, tn) in enumerate(tts):
                        if ti in xn_dve:
                            S.op("dve", [("h", c)], [("xn", c, ti)], lambda: nc.vector.tensor_scalar(
                                out=xn[:, c, t0:t0 + tn], in0=h[:, c, t0:t0 + tn], scalar1=gcol(g, c), scalar2=None,
                                op0=ALU.mult))
                        else:
                            S.op("act", [("h", c)], [("xn", c, ti)], lambda: nc.scalar.activation(
                                out=xn[:, c, t0:t0 + tn], in_=h[:, c, t0:t0 + tn], func=AF.Copy, scale=gcol(g, c)))

            def h_final(d):
                if nstate["pending"] is not None and d > 0:
                    norm_chunk(d - 1, nstate["pending"])

            def rstd_sqrt(grp, bank, res):
                c0, cn = grp
                S.op("act", [res], [res], lambda: nc.scalar.activation(
                    out=bank[:, 0:cn], in_=bank[:, 0:cn], func=AF.Sqrt, bias=EPS, scale=256.0 / D))

            def rstd_recip(gi, grp, bank, res):
                c0, cn = grp
                S.op("dve", [res], [("rstdF", gi)], lambda: nc.vector.reciprocal(
                    out=rstdF[:, c0:c0 + cn], in_=bank[:, 0:cn]))

            def norm_finish():
                spec = nstate["active"]
                g, tts, with_xn = spec
                grps = stat_groups(tts)
                norm_chunk(NCH - 1, spec)
                if not with_xn:
                    k = 0
                    for ti, (t0, tn) in enumerate(tts):
                        for c in range(NCH):
                            if k % 2 == 0:
                                S.op("dve", [("h", c)], [("xn", c, ti)], lambda: nc.vector.tensor_scalar(
                                    out=xn[:, c, t0:t0 + tn], in0=h[:, c, t0:t0 + tn], scalar1=gcol(g, c), scalar2=None,
                                    op0=ALU.mult))
                            else:
                                S.op("act", [("h", c)], [("xn", c, ti)], lambda: nc.scalar.activation(
                                    out=xn[:, c, t0:t0 + tn], in_=h[:, c, t0:t0 + tn], func=AF.Copy, scale=gcol(g, c)))
                            k += 1
                rstd_sqrt(grps[0], SP0, "S0")
                rstd_sqrt(grps[1], SP1, "S1")
                rstd_recip(0, grps[0], SP0, "S0")
                rstd_recip(1, grps[1], SP1, "S1")
                c0, cn = grps[2]
                for c in range(NCH):
                    buf, br = (sqA, sqD)[c % 2][(c // 2) % 2], (("sqA", "sqD")[c % 2], (c // 2) % 2)
                    S.op("act", [("h", c)], [br], lambda: nc.scalar.activation(
                        out=buf[:, 0:cn], in_=h[:, c, c0:c0 + cn], func=AF.Square, scale=0.0625))
                    S.op("pe", [br], ["B0"], lambda: nc.tensor.matmul(
                        PB[0][:, 0:cn], lhsT=ones_h[:], rhs=buf[:, 0:cn], start=(c == 0), stop=(c == NCH - 1)))
                rstd_sqrt(grps[2], PB[0], "B0")
                rstd_recip(2, grps[2], PB[0], "B0")
                nstate["active"] = None

            def h_add(d, pset, pres, tts, scale):
                def emit():
                    last = None
                    for ti, (t0, tn) in enumerate(tts):
                        if scale is None:
                            last = nc.vector.tensor_tensor(out=h[:, d, t0:t0 + tn], in0=pset[ti][:, 0:tn],
                                                           in1=h[:, d, t0:t0 + tn], op=ALU.add)
                        else:
                            last = nc.vector.scalar_tensor_tensor(out=h[:, d, t0:t0 + tn], in0=pset[ti][:, 0:tn],
                                                                  scalar=scale, in1=h[:, d, t0:t0 + tn],
                                                                  op0=ALU.mult, op1=ALU.add)
                    return last
                S.op("dve", [pres, ("h", d)], [("h", d)], emit)

            def out_proj(name, tts):
                for d in range(NCH):
                    ws, wres = wget(name, 0, 0, 16, d * 128)
                    pset, pres = (PA, PA_R) if d % 2 == 0 else (PB, PB_R)
                    mm_group(ws, wres, 16, b2_rhs, tts, pset, pres, [("b2", k) for k in range(NCH)])
                    h_final(d)
                    h_add(d, pset, pres, tts, None)

            def ffn(nm, l, g, tts, hook=None):
                with contextlib.ExitStack() as fs:
                    norm_finish()
                    sg = [fs.enter_context(sbt("sg%d" % i, [128, TH], F32)) for i in range(2)]
                    ut = fs.enter_context(sbt("ut", [128, TH], F32))
                    for (f0, f1) in PIECES:
                        for f in range(f0, f1):
                            ws, wres = wget(nm + "_g", l, 0, 16, f * 128)
                            mm_group(ws, wres, 16, xn_rhs, tts, PA, PA_R, XN_ALL, tile_major=(f == 0))
                            sgb = sg[f % 2]

                            def emit_gs(sgb=sgb):
                                last = None
                                for ti, (t0, tn) in enumerate(tts):
                                    last = nc.vector.tensor_tensor(out=sgb[:, t0:t0 + tn], in0=PA[ti][:, 0:tn],
                                                                   in1=rstdF[:, t0:t0 + tn], op=ALU.mult)
                                return last
                            S.op("dve", [PA_R, RSTD_ALL], [("sg", f % 2)], emit_gs)

                            def emit_silu(sgb=sgb):
                                t0, t1 = tts[0][0], tts[2][0] + tts[2][1]
                                return nc.scalar.activation(out=sgb[:, t0:t1], in_=sgb[:, t0:t1], func=AF.Silu)
                            S.op("act", [("sg", f % 2)], [("sg", f % 2)], emit_silu)
                            ws, wres = wget(nm + "_u", l, 0, 16, f * 128)
                            mm_group(ws, wres, 16, xn_rhs, tts, PB, PB_R, XN_ALL)

                            def emit_us():
                                last = None
                                for ti, (t0, tn) in enumerate(tts):
                                    last = nc.vector.tensor_tensor(out=ut[:, t0:t0 + tn], in0=PB[ti][:, 0:tn],
                                                                   in1=rstdF[:, t0:t0 + tn], op=ALU.mult)
                                return last
                            S.op("dve", [PB_R, RSTD_ALL], ["ut"], emit_us)

                            def emit_hid(sgb=sgb, kk=f - f0):
                                t0, t1 = tts[0][0], tts[2][0] + tts[2][1]
                                return nc.vector.tensor_tensor(out=buf2[:, kk, t0:t1], in0=ut[:, t0:t1],
                                                               in1=sgb[:, t0:t1], op=ALU.mult)
                            S.op("dve", ["ut", ("sg", f % 2)], [("b2", f - f0)], emit_hid)
                            if hook is not None and f == 2:
                                hook(fs)
                        nkp = f1 - f0
                        for d in range(NCH):
                            ws, wres = wget(nm + "_d", l, f0, nkp, d * 128)
                            pset, pres = (PA, PA_R) if d % 2 == 0 else (PB, PB_R)
                            mm_group(ws, wres, nkp, b2_rhs, tts, pset, pres, [("b2", k) for k in range(nkp)], split_last=(d == 0))
                            if f1 == NF:
                                h_final(d)
                            h_add(d, pset, pres, tts, 0.5)
                    S.barrier()

            def load_p(l, pT, ptok):
                pblocks = [(j * 128, 128, j * 128) for j in range(9)] + [(1152, 16, 1152)]
                load_T("p%d" % l, lambda r0, nr: pin[l, r0:r0 + nr, :], pblocks, 2, (pT, "pT"), 0, DPLE, tok=ptok)

            def ple(l, tts, pT):
                with contextlib.ExitStack() as ps_:
                    norm_finish()
                    sig = [ps_.enter_context(sbt("sig%d" % i, [128, TH], F32)) for i in range(2)]
                    tmp = ps_.enter_context(sbt("pletmp", [128, TH], F32))
                    for d in range(NCH):
                        ws, wres = wget("ple_g", l, 0, 16, d * 128)
                        mm_group(ws, wres, 16, xn_rhs, tts, PA, PA_R, XN_ALL, tile_major=(d == 0))
                        h_final(d)
                        sb = sig[d % 2]

                        def emit_sgs(sb=sb):
                            last = None
                            for ti, (t0, tn) in enumerate(tts):
                                last = nc.vector.tensor_tensor(out=sb[:, t0:t0 + tn], in0=PA[ti][:, 0:tn],
                                                               in1=rstdF[:, t0:t0 + tn], op=ALU.mult)
                            return last
                        S.op("dve", [PA_R, RSTD_ALL], [("sig", d % 2)], emit_sgs)

                        def emit_sig(sb=sb):
                            t0, t1 = tts[0][0], tts[2][0] + tts[2][1]
                            return nc.scalar.activation(out=sb[:, t0:t1], in_=sb[:, t0:t1], func=AF.Sigmoid)
                        S.op("act", [("sig", d % 2)], [("sig", d % 2)], emit_sig)
                        ws, wres = wget("ple_p", l, 0, 2, d * 128)
                        mm_group(ws, wres, 2, lambda k, t0, tn: pT[:, k, t0:t0 + tn], tts, PB, PB_R,
                                 [("pT", 0), ("pT", 1)])

                        def emit_pp(sb=sb):
                            last = None
                            for ti, (t0, tn) in enumerate(tts):
                                last = nc.vector.tensor_tensor(out=tmp[:, t0:t0 + tn], in0=PB[ti][:, 0:tn],
                                                               in1=sb[:, t0:t0 + tn], op=ALU.mult)
                            return last
                        S.op("dve", [PB_R, ("sig", d % 2)], ["pletmp"], emit_pp)

                        def emit_hadd(d=d):
                            last = None
                            for ti, (t0, tn) in enumerate(tts):
                                last = nc.vector.tensor_tensor(out=h[:, d, t0:t0 + tn], in0=tmp[:, t0:t0 + tn],
                                                               in1=h[:, d, t0:t0 + tn], op=ALU.add)
                            return last
                        S.op("dve", ["pletmp", ("h", d)], [("h", d)], emit_hadd)
                    S.barrier()

            def gmlp_setup(wmT, bias2, ss):
                    if True:
                        wsn = ss.enter_context(sbt("wsn", [128, 8, 128], F32))
                        bsb = ss.enter_context(sbt("bsb", [128, 8, 128], F32))
                        pt = [v4(SP0), v4(SP1)]
                        S.dma("sp", [], ["wsn"], ("g", "ld", 0), lambda: nc.sync.dma_start(
                            out=wsn[:], in_=a_w_s[0].rearrange("h t s -> t h s")))
                        S.dma("sp", [], ["bsb"], ("g", "ld", 1), lambda: nc.sync.dma_start(
                            out=bsb[:].rearrange("p h t -> p (h t)"),
                            in_=bass.AP(a_b_s.tensor, 0, [[0, 128], [1, 8 * 128]])))
                        for hh in range(8):
                            pq = pt[hh % 2]
                            pr = "S%d" % (hh % 2)
                            S.op("pe", ["wsn"], [pr], lambda hh=hh, pq=pq: nc.tensor.transpose(
                                out=pq[:, 0, :], in_=wsn[:, hh, :], identity=ident_f[:]))
                            S.op("dve", [pr], [("wmT", hh)], lambda hh=hh, pq=pq: nc.vector.tensor_tensor(
                                out=wmT[:, hh, :], in0=pq[:, 0, :], in1=tri_f[:], op=ALU.mult))
                            S.op("pe", [("wmT", hh)], [pr], lambda hh=hh, pq=pq: nc.tensor.matmul(
                                pq[:, 1, :], lhsT=ones_b[:], rhs=wmT[:, hh, :], start=True, stop=True))
                            for c in (2 * hh, 2 * hh + 1):
                                S.op("dve", [pr, "bsb"], [("bias2", c)],
                                     lambda hh=hh, pq=pq, c=c: nc.vector.scalar_tensor_tensor(
                                         out=bias2[:, c, :], in0=pq[:, 1, :], scalar=gcol(G_LNB, c), in1=bsb[:, hh, :],
                                         op0=ALU.mult, op1=ALU.add))

            def gmlp(wmT, bias2):
                with contextlib.ExitStack() as gs:
                    G = gs.enter_context
                    norm_finish()
                    vsf = G(sbt("vsf", [128, NCH, 32], F32))
                    vnf = G(sbt("vnf", [128, NCH, 16], F32))
                    vhb = [G(sbt("vhb%d" % i, [128, D], BF16)) for i in range(2)]
                    st6 = [G(sbt("st6_%d" % i, [128, 4, 6], F32)) for i in range(2)]
                    mv = [G(sbt("mv%d" % i, [128, 2], F32)) for i in range(2)]
                    rt1 = [G(sbt("rt1_%d" % i, [128, 1], F32)) for i in range(2)]
                    rs1 = [G(sbt("rs1_%d" % i, [128, 1], F32)) for i in range(2)]
                    nmr = [G(sbt("nmr%d" % i, [128, 1], F32)) for i in range(2)]
                    ms_ = G(sbt("ms_", [128, 16], F32))
                    msq = G(sbt("msq", [128, 16], F32))
                    var = G(sbt("var", [128, 16], F32))
                    rts = G(sbt("rts", [128, 16], F32))
                    rss = G(sbt("rss", [128, 16], F32))
                    t1 = G(sbt("t1", [128, 16], F32))
                    vso = G(sbt("vso", [128, 1024], F32))
                    ug = [G(sbt("pm%d" % i, [128, TH], F32)) for i in range(2)]
                    for c in range(NCH):
                        ws, wres = wget("a_in", 0, 0, 16, D + c * 128)
                        pset, pres = (PA, PA_R) if c % 2 == 0 else (PB, PB_R)
                        mm_group(ws, wres, 16, xn_rhs, TT_FULL, pset, pres, XN_ALL, tile_major=(c == 0))

                        pmb = ug[c % 2]

                        def emit_vs(pmb=pmb, pset=pset):
                            last = None
                            for ti, (t0, tn) in enumerate(TT_FULL):
                                last = nc.vector.tensor_tensor(out=pmb[:, t0:t0 + tn], in0=pset[ti][:, 0:tn],
                                                               in1=rstdF[:, t0:t0 + tn], op=ALU.mult)
                            return last
                        S.op("dve", [pres, RSTD_ALL], [("ug", c % 2)], emit_vs)

                        def emit_v(c=c, pmb=pmb):
                            nc.scalar.activation(out=buf2[:, c, 0:C_SAMP], in_=pmb[:, 0:C_SAMP], func=AF.Gelu_apprx_tanh)
                            return nc.scalar.activation(out=vsf[:, c, 0:16], in_=pmb[:, C_SAMP:C_SAMP + 16],
                                                        func=AF.Gelu_apprx_tanh)
                        S.op("act", [("ug", c % 2)], [("b2", c), ("vsf", c)], emit_v)
                        S.op("act", [("vsf", c)], [("vsq", c)], lambda c=c: nc.scalar.activation(
                            out=vsf[:, c, 16:32], in_=vsf[:, c, 0:16], func=AF.Square))
                    pvb = [[SP0[:].bitcast(BF16), SP1[:].bitcast(BF16)], [PA[2][:].bitcast(BF16), PB[2][:].bitcast(BF16)]]
                    pvr = [["S0", "S1"], ["A2", "B2"]]
                    def stage_a1(j):
                        c0, jj = j * 128, j % 2
                        pv, pr = pvb[jj], pvr[jj]

                        def emit_tr():
                            last = None
                            for c in range(NCH):
                                last = nc.tensor.transpose(out=pv[c // 8][:, (c % 8) * 128:(c % 8 + 1) * 128],
                                                           in_=buf2[:, c, c0:c0 + 128], identity=ident_b[:])
                            return last
                        S.op("pe", [("b2", c) for c in range(NCH)], [pr], emit_tr)

                        def emit_stats():
                            last = None
                            for q in range(4):
                                last = nc.vector.bn_stats(out=st6[jj][:, q, :], in_=pv[q // 2][:, (q % 2) * 512:(q % 2 + 1) * 512])
                            return last
                        S.op("dve", [pr], [("st6", jj)], emit_stats)
                        S.op("dve", [("st6", jj)], [("mv", jj)], lambda: nc.vector.bn_aggr(
                            out=mv[jj][:], in_=st6[jj][:].rearrange("p a b -> p (a b)")))
                        S.op("act", [("mv", jj)], [("rt1", jj)], lambda: nc.scalar.activation(
                            out=rt1[jj][:], in_=mv[jj][:, 1:2], func=AF.Sqrt, bias=EPS, scale=1.0))

                    def stage_a2(j):
                        jj = j % 2
                        pv, pr = pvb[jj], pvr[jj]
                        S.op("dve", [("rt1", jj)], [("rs1", jj)], lambda: nc.vector.reciprocal(
                            out=rs1[jj][:], in_=rt1[jj][:]))
                        S.op("dve", [("rs1", jj), ("mv", jj)], [("nmr", jj)], lambda: nc.vector.scalar_tensor_tensor(
                            out=nmr[jj][:], in0=mv[jj][:, 0:1], scalar=-1.0, in1=rs1[jj][:], op0=ALU.mult, op1=ALU.mult))
                        vb = vhb[jj]

                        def emit_vh():
                            last = None
                            for q in range(4):
                                last = nc.scalar.activation(out=vb[:, q * 512:(q + 1) * 512],
                                                            in_=pv[q // 2][:, (q % 2) * 512:(q % 2 + 1) * 512],
                                                            func=AF.Identity, bias=nmr[jj][:, 0:1], scale=rs1[jj][:, 0:1])
                            return last
                        S.op("act", [pr, ("nmr", jj), ("rs1", jj)], [("vhb", jj)], emit_vh)

                    def stage_b(j):
                        c0, jj = j * 128, j % 2
                        vb = vhb[jj]
                        for gq in range(2):
                            pset = PA if gq == 0 else PB
                            mres = ["A0", "A1"] if gq == 0 else ["B0", "B1"]

                            def emit_mix(gq=gq, pset=pset):
                                last = None
                                for i in range(8):
                                    c = gq * 8 + i
                                    last = nc.tensor.matmul(pset[i // 4][:, (i % 4) * 128:(i % 4 + 1) * 128],
                                                            lhsT=vb[:, c * 128:(c + 1) * 128], rhs=wmT[:, c // 2, :],
                                                            start=True, stop=True)
                                return last
                            S.op("pe", [("vhb", jj)] + [("wmT", hh) for hh in range(8)], [mres], emit_mix)

                            def emit_sev(gq=gq, pset=pset):
                                last = None
                                for i in range(8):
                                    c = gq * 8 + i
                                    last = nc.vector.scalar_tensor_tensor(
                                        out=buf2[:, c, c0:c0 + 128], in0=pset[i // 4][:, (i % 4) * 128:(i % 4 + 1) * 128],
                                        scalar=gcol(G_LNG, c), in1=bias2[:, c, :], op0=ALU.mult, op1=ALU.add)
                                return last
                            S.op("dve", [mres] + [("bias2", gq * 8 + i) for i in range(8)],
                                 [("b2c", gq * 8 + i, j) for i in range(8)], emit_sev)

                    stage_a1(0)
                    stage_a2(0)
                    for j in range(9):
                        if j + 1 < 9:
                            stage_a1(j + 1)
                            stage_a2(j + 1)
                        stage_b(j)
                    S.op("dve", [("b2c", c, j) for c in range(NCH) for j in range(9)], [("b2", c) for c in range(NCH)] + ["t1"],
                         lambda: nc.vector.memset(t1[:, 0:1], 0.0))

                    def sample_path():
                        pss = SP0
                        pso = v4(SP1)

                        def emit_sst():
                            last = None
                            for c in range(NCH):
                                last = nc.tensor.matmul(pss[:, 0:32], lhsT=ones_f[:], rhs=vsf[:, c, :],
                                                        start=(c == 0), stop=(c == NCH - 1))
                            return last
                        S.op("pe", [("vsf", c) for c in range(NCH)] + [("vsq", c) for c in range(NCH)], ["S0"], emit_sst)
                        S.op("dve", ["S0"], ["ms_"], lambda: nc.vector.tensor_scalar(
                            out=ms_[:], in0=pss[:, 0:16], scalar1=1.0 / D, scalar2=None, op0=ALU.mult))
                        S.op("dve", ["ms_"], ["msq"], lambda: nc.vector.tensor_tensor(
                            out=msq[:], in0=ms_[:], in1=ms_[:], op=ALU.mult))
                        S.op("dve", ["S0", "msq"], ["var"], lambda: nc.vector.scalar_tensor_tensor(
                            out=var[:], in0=pss[:, 16:32], scalar=1.0 / D, in1=msq[:], op0=ALU.mult, op1=ALU.subtract))
                        S.op("act", ["var"], ["rts"], lambda: nc.scalar.activation(
                            out=rts[:], in_=var[:], func=AF.Sqrt, bias=EPS, scale=1.0))
                        S.op("dve", ["rts"], ["rss"], lambda: nc.vector.reciprocal(out=rss[:], in_=rts[:]))
                        for c in range(NCH):
                            S.op("dve", [("vsf", c), "ms_"], ["t1"], lambda c=c: nc.vector.tensor_tensor(
                                out=t1[:], in0=vsf[:, c, 0:16], in1=ms_[:], op=ALU.subtract))
                            S.op("dve", ["t1", "rss"], ["t1"], lambda: nc.vector.tensor_tensor(
                                out=t1[:], in0=t1[:], in1=rss[:], op=ALU.mult))
                            S.op("act", ["t1"], [("vnf", c)], lambda c=c: nc.scalar.activation(
                                out=vnf[:, c, :], in_=t1[:], func=AF.Identity, scale=gcol(G_LNG, c), bias=gcol(G_LNB, c)))
                            S.op("act", [("vnf", c)], [("b2s", c)], lambda c=c: nc.scalar.activation(
                                out=buf2[:, c, C_SAMP:C_SAMP + 16], in_=vnf[:, c, :], func=AF.Identity,
                                scale=w00b[:, c // 2, :], bias=b0b[:, c // 2, :]))
                    def sample_out():
                        pso = v4(SP1)
                        for half in range(2):
                            for q in range(2):
                                def emit_vt(half=half, q=q):
                                    last = None
                                    for cc in range(4):
                                        c = half * 8 + q * 4 + cc
                                        last = nc.tensor.transpose(out=pso[0:16, cc, :], in_=vnf[:, c, :], identity=ident_f[:])
                                    return last
                                S.op("pe", [("vnf", c) for c in range(NCH)], ["S1"], emit_vt)
                                S.op("act", ["S1"], ["vso"], lambda q=q: nc.scalar.activation(
                                    out=vso[0:16, q * 512:(q + 1) * 512], in_=pso[0:16, :, :].rearrange("p a b -> p (a b)"),
                                    func=AF.Copy))
                            S.dma("sp", ["vso"], [], ("o", "vs"), lambda half=half: nc.sync.dma_start(
                                out=vs_out[:, half * 1024:(half + 1) * 1024], in_=vso[0:16, :]))

                    sample_path()
                    for c in range(NCH):
                        ws, wres = wget("a_in", 0, 0, 16, c * 128)
                        pset, pres = (PA, PA_R) if c % 2 == 0 else (PB, PB_R)
                        mm_group(ws, wres, 16, xn_rhs, TT, pset, pres, XN_ALL)
                        if c == 5:
                            sample_out()
                        ub = ug[c % 2]

                        def emit_us(ub=ub, pset=pset):
                            last = None
                            for ti, (t0, tn) in enumerate(TT):
                                last = nc.vector.tensor_tensor(out=ub[:, t0:t0 + tn], in0=pset[ti][:, 0:tn],
                                                               in1=rstdF[:, t0:t0 + tn], op=ALU.mult)
                            return last
                        S.op("dve", [pres, RSTD_ALL], [("ug", c % 2)], emit_us)
                        S.op("act", [("ug", c % 2)], [("ug", c % 2)], lambda ub=ub: nc.scalar.activation(
                            out=ub[:, 126:TH], in_=ub[:, 126:TH], func=AF.Gelu_apprx_tanh))
                        S.op("dve", [("ug", c % 2), ("b2", c), ("b2s", c)], [("b2", c)], lambda ub=ub, c=c: nc.vector.tensor_tensor(
                            out=buf2[:, c, 126:TH], in0=ub[:, 126:TH], in1=buf2[:, c, 126:TH], op=ALU.mult))
                    out_proj("a_out", TT)
                    S.barrier()

            def load_state(stT, fs):
                stok = [fs.enter_context(sbt("stok%d" % i, [128, D], F32)) for i in range(2)]
                for r in range(2):
                    load_T("st%d" % r, lambda r0, nr, r=r: stc[:, r, :], [(0, 16, 0)], NCH, (stT, "stT"), r * 16, D,
                           tok=[stok[r], stok[r]])

            def sconv(stT):
                with contextlib.ExitStack() as cs:
                    C = cs.enter_context
                    norm_finish()
                    rstd2 = C(sbt("rstd2", [128, TH], F32))
                    S.op("dve", [RSTD_ALL], ["rstd2"], lambda: nc.vector.tensor_tensor(
                        out=rstd2[:, 126:TH], in0=rstdF[:, 126:TH], in1=rstdF[:, 126:TH], op=ALU.mult))
                    cgs = C(sbt("cgs", [128, TH], F32))
                    cib = [C(sbt("cib%d" % i, [128, 1026], F32)) for i in range(2)]
                    coe = C(sbt("coe", [128, 1042], F32))
                    cit = C(sbt("cit", [128, NCH, 18], F32))
                    for c in range(NCH):
                        w0, w1, w2 = gcol(G_WC, c), gcol(G_WC + 1, c), gcol(G_WC + 2, c)
                        if c % 2 == 0:
                            PX, pxr, PY, pyr = PA, PA_R, PB, PB_R
                        else:
                            PX, pxr, PY, pyr = PB, PB_R, PA, PA_R
                        ws, wres = wget("c_in", 0, 0, 16, D + c * 128)
                        mm_group(ws, wres, 16, xn_rhs, TT, PX, pxr, XN_ALL, tile_major=(c == 0))

                        def emit_cg(PX=PX):
                            last = None
                            for ti, (t0, tn) in enumerate(TT):
                                last = nc.vector.tensor_tensor(out=cgs[:, t0:t0 + tn], in0=PX[ti][:, 0:tn],
                                                               in1=rstd2[:, t0:t0 + tn], op=ALU.mult)
                            return last
                        S.op("dve", [pxr, "rstd2"], ["cgs"], emit_cg)
                        ws, wres = wget("c_in", 0, 0, 16, 2 * D + c * 128)
                        mm_group(ws, wres, 16, xn_rhs, TT, PY, pyr, XN_ALL)
                        cb = cib[c % 2]

                        def emit_ci(cb=cb, c=c, PY=PY):
                            last = None
                            for ti, (t0, tn) in enumerate(TT):
                                nm = min(t0 + tn, C_SAMP) - t0
                                last = nc.vector.tensor_tensor(out=cb[:, t0 - 126:t0 - 126 + nm], in0=PY[ti][:, 0:nm],
                                                               in1=cgs[:, t0:t0 + nm], op=ALU.mult)
                                if nm < tn:
                                    last = nc.vector.tensor_tensor(out=cit[:, c, 2:18], in0=PY[ti][:, nm:tn],
                                                                   in1=cgs[:, C_SAMP:C_SAMP + 16], op=ALU.mult)
                            return last
                        S.op("dve", [pyr, "cgs"], [("cib", c % 2), ("cit", c)], emit_ci)

                        def emit_cp(cb=cb, c=c):
                            nc.vector.tensor_copy(out=cit[:, c, 0:2], in_=cb[:, 1024:1026])
                            nc.vector.tensor_copy(out=coe[:, 0:2], in_=cb[:, 0:2])
                            nc.vector.tensor_scalar(out=coe[:, 1026:1042], in0=stT[:, c, :], scalar1=w0, scalar2=None,
                                                    op0=ALU.mult)
                            return nc.vector.tensor_scalar(out=coe[:, 2:1026], in0=cb[:, 0:1024], scalar1=w0, scalar2=None,
                                                           op0=ALU.mult)
                        S.op("dve", [("cib", c % 2)] + [("stT", k) for k in range(32)], ["coe", ("cit", c)], emit_cp)

                        def emit_c1(cb=cb, c=c):
                            nc.vector.scalar_tensor_tensor(out=coe[:, 1026:1042], in0=stT[:, 16 + c, :], scalar=w1,
                                                           in1=coe[:, 1026:1042], op0=ALU.mult, op1=ALU.add)
                            return nc.vector.scalar_tensor_tensor(out=coe[:, 2:1026], in0=cb[:, 1:1025], scalar=w1,
                                                                  in1=coe[:, 2:1026], op0=ALU.mult, op1=ALU.add)
                        S.op("dve", [("cib", c % 2), "coe"], ["coe"], emit_c1)

                        def emit_c2(cb=cb, c=c):
                            nc.vector.scalar_tensor_tensor(out=coe[:, 1026:1042], in0=cit[:, c, 2:18], scalar=w2,
                                                           in1=coe[:, 1026:1042], op0=ALU.mult, op1=ALU.add)
                            return nc.vector.scalar_tensor_tensor(out=coe[:, 2:1026], in0=cb[:, 2:1026], scalar=w2,
                                                                  in1=coe[:, 2:1026], op0=ALU.mult, op1=ALU.add)
                        S.op("dve", [("cib", c % 2), ("cit", c), "coe"], ["coe"], emit_c2)
                        S.op("dve", ["coe", RSTD_ALL], ["coe"], lambda: nc.vector.tensor_tensor(
                            out=coe[:, 0:1042], in0=coe[:, 0:1042], in1=rstdF[:, 126:TH], op=ALU.mult))
                        ws, wres = wget("c_in", 0, 0, 16, c * 128)
                        mm_group(ws, wres, 16, xn_rhs, TT, PX, pxr, XN_ALL)

                        def emit_yc(c=c, PX=PX):
                            last = None
                            for ti, (t0, tn) in enumerate(TT):
                                last = nc.vector.tensor_tensor(out=buf2[:, c, t0:t0 + tn], in0=PX[ti][:, 0:tn],
                                                               in1=coe[:, t0 - 126:t0 - 126 + tn], op=ALU.mult)
                            return last
                        S.op("dve", [pxr, "coe"], [("b2", c)], emit_yc)
                    with contextlib.ExitStack() as c2:
                        pso = v4(SP1)
                        cio = c2.enter_context(sbt("cio", [128, 1024], F32))
                        S.dma("sp", [], [], ("o", "ncs0"), lambda: nc.sync.dma_start(out=ncs_out[:, 0, :], in_=stc[:, 1, :]))
                        for half in range(2):
                            for q in range(2):
                                def emit_ct(half=half, q=q):
                                    last = None
                                    for cc in range(4):
                                        c = half * 8 + q * 4 + cc
                                        last = nc.tensor.transpose(out=pso[0:18, cc, :], in_=cit[:, c, :], identity=ident_f[:])
                                    return last
                                S.op("pe", [("cit", c) for c in range(NCH)], ["S1"], emit_ct)
                                S.op("act", ["S1"], ["cio"], lambda q=q: nc.scalar.activation(
                                    out=cio[0:18, q * 512:(q + 1) * 512], in_=pso[0:18, :, :].rearrange("p a b -> p (a b)"),
                                    func=AF.Copy))
                            S.dma("sp", ["cio"], [], ("o", "ncp"), lambda half=half: nc.sync.dma_start(
                                out=ncp_out[:, half * 1024:(half + 1) * 1024], in_=cio[0:2, :]))
                            S.dma("sp", ["cio"], [], ("o", "ncs1"), lambda half=half: nc.sync.dma_start(
                                out=ncs_out[:, 1, half * 1024:(half + 1) * 1024], in_=cio[2:18, :]))
                        S.barrier()
                    out_proj("c_out", TT)
                    S.barrier()

            def layer_tail(l, after_spec):
                with contextlib.ExitStack() as ts:
                    pT = ts.enter_context(sbt("pT", [128, 2, TH], BF16))
                    ptok = [ts.enter_context(sbt("ptok%d" % i, [128, DPLE], F32)) for i in range(2)]
                    run_phase(lambda: ffn("ffn2", l, G_FFN2, TT, hook=lambda fs: load_p(l, pT, ptok)),
                              (G_PLE + l, TT, True))
                    run_phase(lambda: ple(l, TT, pT), after_spec)
                    S.barrier()

            def run_phase(fn, next_spec):
                nstate["pending"] = next_spec
                fn()
                nstate["active"] = next_spec
                nstate["pending"] = None

            first = (G_FFN1 + 0, TT_FULL, True)
            nstate["active"] = first
            for c in range(NCH - 1):
                norm_chunk(c, first, xn_dve=(1, 2))
            with contextlib.ExitStack() as g0:
                wmT = g0.enter_context(sbt("wmT", [128, 8, 128], BF16))
                bias2 = g0.enter_context(sbt("bias2", [128, NCH, 128], F32))
                run_phase(lambda: ffn("ffn1", 0, G_FFN1, TT_FULL, hook=lambda fs: gmlp_setup(wmT, bias2, fs)),
                          (G_MIX + 0, TT_FULL, True))
                run_phase(lambda: gmlp(wmT, bias2), (G_FFN2 + 0, TT, True))
                S.barrier()
            layer_tail(0, (G_FFN1 + 1, TT, False))
            with contextlib.ExitStack() as c0s:
                stT = c0s.enter_context(sbt("stT", [128, 32, 16], F32))
                run_phase(lambda: ffn("ffn1", 1, G_FFN1, TT, hook=lambda fs: load_state(stT, fs)), (G_MIX + 1, TT, True))
                run_phase(lambda: sconv(stT), (G_FFN2 + 1, TT, True))
                S.barrier()
            layer_tail(1, None)
            assert wstate["used"] == len(plan)
            S.barrier()

        with contextlib.ExitStack() as fs:
            Fz = fs.enter_context
            gfb = Fz(sbt("gfb", [128, D], F32))
            yo = [Fz(sbt("yo%d" % i, [128, D], F32)) for i in range(2)]
            pf = [banks[0:4], banks[4:8]]
            fjunk = Fz(sbt("fjunk", [128, 4, 512], BF16))
            ssq = [Fz(sbt("fssq%d" % i, [128, 4], F32)) for i in range(2)]
            ms2 = [Fz(sbt("fms2_%d" % i, [128, 1], F32)) for i in range(2)]
            rt = [Fz(sbt("frt%d" % i, [128, 1], F32)) for i in range(2)]
            rs = [Fz(sbt("frs%d" % i, [128, 1], F32)) for i in range(2)]
            S.dma("sp", [], ["gfb"], ("f", "g"), lambda: nc.sync.dma_start(
                out=gfb[:], in_=bass.AP(final_norm.tensor, 0, [[0, 128], [1, D]])))
            fblocks = [(C_MAIN + j * 128, 128, j * 128, 0) for j in range(8)] + [(C_SAMP, 16, 1024, 0)]
            for bi, (c0, n, row0, skip) in enumerate(fblocks):
                pfb = pf[bi % 2]
                yb = yo[bi % 2]

                def emit_ft(pfb=pfb, c0=c0, n=n):
                    last = None
                    for c in range(NCH):
                        last = nc.tensor.transpose(out=pfb[c // 4][0:n, (c % 4) * 128:(c % 4 + 1) * 128],
                                                   in_=h[:, c, c0:c0 + n], identity=ident_f[:])
                    return last
                S.op("pe", [("h", c) for c in range(NCH)], [("pf", bi % 2)], emit_ft)

                bb = bi % 2

                def emit_fs(pfb=pfb, n=n, bb=bb):
                    last = None
                    for q in range(4):
                        last = nc.scalar.activation(out=fjunk[0:n, q, :], in_=pfb[q][0:n, :], func=AF.Square,
                                                    accum_out=ssq[bb][0:n, q:q + 1])
                    return last
                S.op("act", [("pf", bb)], [("fssq", bb), "fjunk"], emit_fs)
                S.op("dve", [("fssq", bb)], [("fms2", bb)], lambda n=n, bb=bb: nc.vector.tensor_reduce(
                    out=ms2[bb][0:n, :], in_=ssq[bb][0:n, :], axis=mybir.AxisListType.X, op=ALU.add))
                S.op("act", [("fms2", bb)], [("frt", bb)], lambda n=n, bb=bb: nc.scalar.activation(
                    out=rt[bb][0:n, :], in_=ms2[bb][0:n, :], func=AF.Sqrt, bias=EPS, scale=1.0 / D))
                S.op("dve", [("frt", bb)], [("frs", bb)], lambda n=n, bb=bb: nc.vector.reciprocal(
                    out=rs[bb][0:n, :], in_=rt[bb][0:n, :]))

                def emit_fo(pfb=pfb, yb=yb, n=n, bb=bb):
                    last = None
                    for q in range(4):
                        last = nc.vector.scalar_tensor_tensor(out=yb[0:n, q * 512:(q + 1) * 512],
                                                              in0=pfb[q][0:n, :], scalar=rs[bb][0:n, 0:1],
                                                              in1=gfb[0:n, q * 512:(q + 1) * 512], op0=ALU.mult, op1=ALU.mult)
                    return last
                S.op("dve", [("pf", bi % 2), ("frs", bb), "gfb"], [("yo", bi % 2)], emit_fo)
                S.dma("sp", [("yo", bi % 2)], [], ("o", "y", bi % 2),
                      lambda yb=yb, n=n, row0=row0, skip=skip: nc.sync.dma_start(
                          out=y_out[row0:row0 + n - skip, :], in_=yb[skip:n, :]))
            S.finish()
    return nc


_CACHE = {}


def _get_program():
    if "nc" not in _CACHE:
        _CACHE["nc"] = build_program()
    return _CACHE["nc"]


def kernel(x_prompt, x_sample, state_conv, p_prompt, p_sample,
           ffn1_norm, ffn1_w_gate, ffn1_w_up, ffn1_w_down,
           mix_norm, a_w_in, a_ln_g, a_ln_b, a_w_s, a_b_s, a_w_out,
           c_w_in, c_w_conv, c_w_out,
           ffn2_norm, ffn2_w_gate, ffn2_w_up, ffn2_w_down,
           ple_norm, ple_w_gate, ple_w_proj, final_norm):
    f = lambda a: np.ascontiguousarray(np.asarray(a, dtype=np.float32))
    x_prompt, x_sample, state_conv, p_prompt, p_sample = map(f, (x_prompt, x_sample, state_conv, p_prompt, p_sample))

    def colform(v):
        return f(v).reshape(16, 128).T

    groups = [None] * NGRP
    for l in range(2):
        groups[G_FFN1 + l] = colform(ffn1_norm[l])
        groups[G_MIX + l] = colform(mix_norm[l])
        groups[G_FFN2 + l] = colform(ffn2_norm[l])
        groups[G_PLE + l] = colform(ple_norm[l])
    groups[G_LNG] = colform(a_ln_g[0])
    groups[G_LNB] = colform(a_ln_b[0])
    for k in range(3):
        groups[G_WC + k] = colform(c_w_conv[0, k])
    cols = np.ascontiguousarray(np.concatenate(groups, axis=1))
    ident = np.eye(128, dtype=np.float32)
    tri = np.triu(np.ones((128, 128), dtype=np.float32))

    shared = {
        "cols": cols, "ident": ident, "tri": tri,
        "ffn1_w_gate": f(ffn1_w_gate), "ffn1_w_up": f(ffn1_w_up), "ffn1_w_down": f(ffn1_w_down),
        "ffn2_w_gate": f(ffn2_w_gate), "ffn2_w_up": f(ffn2_w_up), "ffn2_w_down": f(ffn2_w_down),
        "a_w_in": f(a_w_in), "a_w_out": f(a_w_out), "c_w_in": f(c_w_in), "c_w_out": f(c_w_out),
        "ple_w_gate": f(ple_w_gate), "ple_w_proj": f(ple_w_proj),
        "a_w_s": f(a_w_s), "a_b_s": f(a_b_s), "final_norm": f(final_norm),
    }
    in_maps = []
    for c in range(N_CORES):
        b, half = c // 2, c % 2
        xin = np.zeros((TH, D), np.float32)
        pin = np.zeros((2, TH, DPLE), np.float32)
        xin[128:1152] = x_prompt[b, half * 1024:(half + 1) * 1024]
        pin[:, 128:1152] = p_prompt[:, b, half * 1024:(half + 1) * 1024]
        if half == 1:
            xin[0:128] = x_prompt[b, 896:1024]
            pin[:, 0:128] = p_prompt[:, b, 896:1024]
        xin[1152:1168] = x_sample[c * 16:(c + 1) * 16, 0]
        pin[:, 1152:1168] = p_sample[:, c * 16:(c + 1) * 16, 0]
        m = dict(shared)
        m["xin"] = xin
        m["pin"] = pin
        m["stc"] = np.ascontiguousarray(state_conv[0, c * 16:(c + 1) * 16])
        in_maps.append(m)

    nc = _get_program()
    res = run_bass_kernel_spmd(nc, in_maps, core_ids=list(range(N_CORES)))
    r = res.results
    y_prompt = np.zeros((4, 2048, D), np.float32)
    y_sample = np.zeros((128, 1, D), np.float32)
    ncp = np.zeros((1, 4, 2, D), np.float32)
    ncs = np.zeros((1, 128, 2, D), np.float32)
    vs = np.zeros((1, 128, 1, D), np.float32)
    for c in range(N_CORES):
        b, half = c // 2, c % 2
        y_prompt[b, half * 1024:(half + 1) * 1024] = r[c]["y"][0:1024]
        y_sample[c * 16:(c + 1) * 16, 0] = r[c]["y"][1024:1040]
        if half == 1:
            ncp[0, b] = r[c]["ncp"]
        ncs[0, c * 16:(c + 1) * 16] = r[c]["ncs"]
        vs[0, c * 16:(c + 1) * 16, 0] = r[c]["vs"]
    return (y_prompt, y_sample, ncp, ncs, vs)
```
